# Optimizing a Trainium2 kernel written in Bass

```python
import jax, jax.numpy as jnp
from jax import lax
import numpy as np

D_MODEL = 1024
BATCH = 16
SEQ = 2048
DEPTH = 2

CHUNK = 64
N_A_LAYERS = DEPTH // 2
N_B_LAYERS = DEPTH - N_A_LAYERS
GMLP_BLOCK = 128
GMLP_WIDTH = D_MODEL
GMLP_GROUPS = 8
GMLP_GROUP_DIM = GMLP_WIDTH // GMLP_GROUPS
N_HEADS = 16
HEAD_DIM = D_MODEL // N_HEADS
LEFT_CHUNKS = 8
BAND = (LEFT_CHUNKS + 1) * CHUNK
PAD = LEFT_CHUNKS * CHUNK
REL_CLIP = 128
REL_SIZE = 2 * REL_CLIP + 1
D_FF = ((8 * D_MODEL // 3 + 127) // 128) * 128
CONV_WIDTH = 3
EPS = 1e-6
NEG_INF = -1e30

kernel_name = "hybrid_gmlp_chunkattn_yoco_convffn"


def rmsnorm(x, g):
    xf = x.astype(jnp.float32)
    y = xf * lax.rsqrt(jnp.mean(xf * xf, axis=-1, keepdims=True) + EPS)
    return (y * g.astype(jnp.float32)).astype(x.dtype)


def gmlp_mixer(h, w_in, v_norm_g, w_s, b_s, w_out):
    B, S, _ = h.shape
    z = jax.nn.gelu(h @ w_in)
    u, v = jnp.split(z, 2, axis=-1)
    v = rmsnorm(v, v_norm_g)
    pos_chunk = jnp.arange(GMLP_BLOCK) // CHUNK
    mask = pos_chunk[:, None] >= pos_chunk[None, :]
    w = jnp.where(mask[None], w_s, 0)
    v = v.reshape(B, S // GMLP_BLOCK, GMLP_BLOCK, GMLP_GROUPS, GMLP_GROUP_DIM)
    s = jnp.einsum('gij,bnjgc->bnigc', w, v) + b_s.T[None, None, :, :, None]
    out = u * s.reshape(B, S, GMLP_WIDTH)
    return out @ w_out


def chunk_attention(h, k, v, w_q, rel_bias, w_o):
    B, S, _ = h.shape
    nc = S // CHUNK
    scale = HEAD_DIM ** -0.5
    q = (h @ w_q).reshape(B, nc, CHUNK, N_HEADS, HEAD_DIM) * scale
    qc = jnp.moveaxis(q, 1, 0)
    kp = jnp.pad(k, ((0, 0), (PAD, 0), (0, 0), (0, 0)))
    vp = jnp.pad(v, ((0, 0), (PAD, 0), (0, 0), (0, 0)))
    qi = jnp.arange(CHUNK)[:, None]
    kj = jnp.arange(BAND)[None, :]
    rel_idx = jnp.clip(qi - kj + PAD, -REL_CLIP, REL_CLIP) + REL_CLIP
    bias = rel_bias[:, rel_idx].astype(jnp.float32)

    def one_chunk(args):
        c, qb = args
        start = c * CHUNK
        kb = lax.dynamic_slice_in_dim(kp, start, BAND, axis=1)
        vb = lax.dynamic_slice_in_dim(vp, start, BAND, axis=1)
        sc = jnp.einsum('bqhd,bkhd->bhqk', qb, kb).astype(jnp.float32) + bias
        valid = (start - PAD + jnp.arange(BAND)) >= 0
        sc = jnp.where(valid[None, None, None, :], sc, NEG_INF)
        p = jax.nn.softmax(sc, axis=-1).astype(vb.dtype)
        return jnp.einsum('bhqk,bkhd->bqhd', p, vb)

    o = lax.map(one_chunk, (jnp.arange(nc), qc))
    o = jnp.moveaxis(o, 0, 1).reshape(B, S, N_HEADS * HEAD_DIM)
    return o @ w_o


def conv_ffn(h, w_in, conv_w, conv_b, w_down):
    a = h @ w_in
    C = a.shape[-1]
    a = lax.conv_general_dilated(
        a, conv_w[:, None, :].astype(a.dtype), window_strides=(1,),
        padding=[(CONV_WIDTH - 1, 0)], dimension_numbers=('NWC', 'WIO', 'NWC'),
        feature_group_count=C) + conv_b
    up, gate = jnp.split(a, 2, axis=-1)
    return (jax.nn.silu(gate) * up) @ w_down


def setup_inputs(seed: int = 0) -> dict:
    key = jax.random.key(seed)
    ks = jax.random.split(key, 20)
    nrm = lambda k, shape, s: jax.random.normal(k, shape, jnp.float32) * s
    gain = lambda k, shape: 1.0 + nrm(k, shape, 0.02)
    HD = N_HEADS * HEAD_DIM
    return {
        "x": nrm(ks[0], (BATCH, SEQ, D_MODEL), 1.0),
        "a_norm_g": gain(ks[1], (N_A_LAYERS, D_MODEL)),
        "a_w_in": nrm(ks[2], (N_A_LAYERS, D_MODEL, 2 * GMLP_WIDTH), D_MODEL ** -0.5),
        "a_v_norm_g": gain(ks[3], (N_A_LAYERS, GMLP_WIDTH)),
        "a_w_s": nrm(ks[4], (N_A_LAYERS, GMLP_GROUPS, GMLP_BLOCK, GMLP_BLOCK), GMLP_BLOCK ** -0.5),
        "a_b_s": 1.0 + nrm(ks[5], (N_A_LAYERS, GMLP_GROUPS, GMLP_BLOCK), 0.01),
        "a_w_out": nrm(ks[6], (N_A_LAYERS, GMLP_WIDTH, D_MODEL), GMLP_WIDTH ** -0.5),
        "kv_norm_g": gain(ks[7], (D_MODEL,)),
        "w_kv": nrm(ks[8], (D_MODEL, 2 * HD), D_MODEL ** -0.5),
        "b_norm_g": gain(ks[9], (N_B_LAYERS, D_MODEL)),
        "b_w_q": nrm(ks[10], (N_B_LAYERS, D_MODEL, HD), D_MODEL ** -0.5),
        "b_rel_bias": nrm(ks[11], (N_B_LAYERS, N_HEADS, REL_SIZE), 0.5),
        "b_w_o": nrm(ks[12], (N_B_LAYERS, HD, D_MODEL), HD ** -0.5),
        "f_norm_g": gain(ks[13], (DEPTH, D_MODEL)),
        "f_w_in": nrm(ks[14], (DEPTH, D_MODEL, 2 * D_FF), D_MODEL ** -0.5),
        "f_conv_w": nrm(ks[15], (DEPTH, CONV_WIDTH, 2 * D_FF), CONV_WIDTH ** -0.5),
        "f_conv_b": nrm(ks[16], (DEPTH, 2 * D_FF), 0.01),
        "f_w_down": nrm(ks[17], (DEPTH, D_FF, D_MODEL), D_FF ** -0.5),
        "final_norm_g": gain(ks[18], (D_MODEL,)),
    }


def reference(x, a_norm_g, a_w_in, a_v_norm_g, a_w_s, a_b_s, a_w_out,
              kv_norm_g, w_kv, b_norm_g, b_w_q, b_rel_bias, b_w_o,
              f_norm_g, f_w_in, f_conv_w, f_conv_b, f_w_down, final_norm_g):
    B, S, _ = x.shape
    h = x
    k_shared = v_shared = None
    for l in range(DEPTH):
        if l < N_A_LAYERS:
            h = h + gmlp_mixer(rmsnorm(h, a_norm_g[l]), a_w_in[l], a_v_norm_g[l],
                               a_w_s[l], a_b_s[l], a_w_out[l])
        else:
            if l == N_A_LAYERS:
                kv = rmsnorm(h, kv_norm_g) @ w_kv
                k_shared, v_shared = jnp.split(kv, 2, axis=-1)
                k_shared = k_shared.reshape(B, S, N_HEADS, HEAD_DIM)
                v_shared = v_shared.reshape(B, S, N_HEADS, HEAD_DIM)
            j = l - N_A_LAYERS
            h = h + chunk_attention(rmsnorm(h, b_norm_g[j]), k_shared, v_shared,
                                    b_w_q[j], b_rel_bias[j], b_w_o[j])
        h = h + conv_ffn(rmsnorm(h, f_norm_g[l]), f_w_in[l], f_conv_w[l],
                         f_conv_b[l], f_w_down[l])
    return rmsnorm(h, final_norm_g)
```

```python
import numpy as np
from contextlib import ExitStack
import concourse.bass as bass
import concourse.mybir as mybir
from concourse.bass_utils import run_bass_kernel_spmd

F32 = mybir.dt.float32
BF16 = mybir.dt.bfloat16
AF = mybir.ActivationFunctionType
ALU = mybir.AluOpType

ENGS = ["pe", "act", "dve", "pool", "sp"]
D = 1024
NTOK = 4096
SEQ = 2048
DFF = 2816
EPS = 1e-6
SB_BYTES = 212000
N_CORES = 8
DEBUG_NT = None
NSTAGE = 6


class Res:
    __slots__ = ("name", "w", "r", "dsem")

    def __init__(self, name=""):
        self.name = name
        self.w = None
        self.r = {}
        self.dsem = None


class Sched:
    def __init__(self, nc, es):
        self.nc = nc
        self.es = es
        self.q = {e: [] for e in ENGS}
        self.cnt = {e: 0 for e in ENGS}
        self.known = {e: {} for e in ENGS}
        self.sem = {}
        self.dsems = []
        self.pe_pending = []
        for e in ["pe", "act", "dve", "pool"]:
            self.sem[e] = es.enter_context(nc.semaphore("s_" + e))

    def new_dsem(self):
        key = "d%d" % len(self.dsems)
        self.sem[key] = self.es.enter_context(self.nc.semaphore("s_" + key))
        ds = [key, 0]
        self.dsems.append(ds)
        return ds

    def _wait(self, eng, ev):
        key, val = ev
        if self.known[eng].get(key, 0) >= val:
            return
        self.known[eng][key] = val
        self.q[eng].append(("wait", key, val))

    def _deps(self, eng, reads, writes, is_dma):
        for r in reads:
            if r.w is not None:
                if is_dma or r.w[0] != eng or eng != "pe":
                    self._wait(eng, r.w)
        for w in writes:
            if w.w is not None and (is_dma or w.w[0] != eng):
                self._wait(eng, w.w)
            for k, v in w.r.items():
                if is_dma or k != eng:
                    self._wait(eng, (k, v))

    def _record(self, ev, reads, writes):
        for r in reads:
            if r.r.get(ev[0], 0) < ev[1]:
                r.r[ev[0]] = ev[1]
        for w in writes:
            w.w = ev
            w.r = {}

    def op(self, eng, fn, reads=(), writes=(), signal=True):
        if eng != "pe":
            assert not self.pe_pending, "non-PE op inside an unsignaled PE group"
        self._deps(eng, reads, writes, False)
        if eng == "pe" and not signal:
            self.q[eng].append(("ins", fn, None, 0))
            self.pe_pending.append((tuple(reads), tuple(writes)))
            return None
        self.cnt[eng] += 1
        ev = (eng, self.cnt[eng])
        self.q[eng].append(("ins", fn, eng, 1))
        if eng == "pe":
            for rr, ww in self.pe_pending:
                self._record(ev, rr, ww)
            self.pe_pending = []
        self._record(ev, reads, writes)
        return ev

    def dma(self, eng, out, in_, owner, reads=(), writes=(), noncontig=False):
        assert not self.pe_pending
        self._deps(eng, reads, writes, True)
        if owner.dsem is None:
            owner.dsem = self.new_dsem()
        owner.dsem[1] += 16
        ev = (owner.dsem[0], owner.dsem[1])
        if noncontig:
            fn = lambda e: e.dma_start(out=out, in_=in_, allow_slow_non_contiguous=True)
        else:
            fn = lambda e: e.dma_start(out=out, in_=in_)
        self.q[eng].append(("ins", fn, ev[0], 16))
        self._record(ev, reads, writes)
        return ev

    def barrier(self):
        assert not self.pe_pending
        for eng in ENGS:
            for f in ["pe", "act", "dve", "pool"]:
                if f != eng and self.cnt[f] > 0:
                    self._wait(eng, (f, self.cnt[f]))
            for ds in self.dsems:
                if ds[1] > 0:
                    self._wait(eng, (ds[0], ds[1]))

    def replay(self, block):
        engmap = {"pe": block.tensor, "act": block.scalar, "dve": block.vector,
                  "pool": block.gpsimd, "sp": block.sync}
        sem = self.sem
        for name in ENGS:
            items = self.q[name]

            def body(e, items=items):
                for it in items:
                    if it[0] == "wait":
                        e.wait_ge(sem[it[1]], it[2])
                    else:
                        ins = it[1](e)
                        if it[2] is not None:
                            ins.then_inc(sem[it[2]], it[3])
            engmap[name](body)


class Builder:
    def __init__(self, nc, es):
        self.nc = nc
        self.es = es
        self.S = Sched(nc, es)
        self.big = es.enter_context(nc.sbuf_tensor("big", [128, SB_BYTES // 2], BF16))
        self.ps = es.enter_context(nc.psum_tensor("ps", [128, 4096], F32))
        self.psb = self.ps.bitcast(BF16)
        self.off = 0
        self.dram = {}
        self.identf = self.alloc([128], F32)
        self.ident = self.alloc([128], BF16)
        self.neghalf = self.alloc([2], F32)
        R = Res("const")
        S = self.S
        identf, ident, neghalf = self.identf, self.ident, self.neghalf
        S.op("pool", lambda e: e.memset(identf, 0.0), writes=[R])
        S.op("pool", lambda e: e.affine_select(out=identf, in_=identf, pattern=[[-1, 128]],
                                               compare_op=ALU.not_equal, fill=1.0, base=0,
                                               channel_multiplier=1), reads=[R], writes=[R])
        S.op("dve", lambda e: e.tensor_copy(out=ident, in_=identf), reads=[R], writes=[R])
        S.op("pool", lambda e: e.memset(neghalf, -0.5), writes=[R])
        self.const_base = self.off

    def alloc(self, shape, dtype):
        n = int(np.prod(shape))
        nbytes = n * (4 if dtype == F32 else 2)
        start = self.off
        self.off += (nbytes + 63) // 64 * 64
        assert self.off <= SB_BYTES, "SBUF overflow: %d" % self.off
        ap = self.big[:, start // 2: start // 2 + nbytes // 2]
        if dtype == F32:
            ap = ap.bitcast(F32)
        if len(shape) == 2:
            ap = ap.rearrange("p (a b) -> p a b", a=shape[0], b=shape[1])
        elif len(shape) == 3:
            ap = ap.rearrange("p (a b c) -> p a b c", a=shape[0], b=shape[1], c=shape[2])
        return ap

    def din(self, name, shape, dtype=F32):
        t = self.nc.dram_tensor(name, list(shape), dtype, kind="ExternalInput").ap()
        self.dram[name] = t
        return t

    def bank(self, b, n=512):
        return self.ps[:, b * 512: b * 512 + n]

    def ACT(self, out, in_, func, reads, writes, **kw):
        self.S.op("act", lambda e: e.activation(out=out, in_=in_, func=func, **kw), reads, writes)

    def MM(self, out, lhsT, rhs, start, stop, reads, writes, signal):
        self.S.op("pe", lambda e: e.matmul(out, lhsT=lhsT, rhs=rhs, start=start, stop=stop),
                  reads, writes, signal=signal)

    def TR(self, out, in_, reads, writes, signal):
        ident = self.ident
        self.S.op("pe", lambda e: e.transpose(out=out, in_=in_, identity=ident), reads, writes,
                  signal=signal)

    def TT(self, eng, out, in0, in1, op, reads, writes):
        self.S.op(eng, lambda e: e.tensor_tensor(out=out, in0=in0, in1=in1, op=op), reads, writes)

    def TS(self, eng, out, in0, s1, s2, op0, op1, reads, writes):
        if s2 is None:
            self.S.op(eng, lambda e: e.tensor_scalar(out=out, in0=in0, scalar1=s1, scalar2=None,
                                                     op0=op0), reads, writes)
        else:
            self.S.op(eng, lambda e: e.tensor_scalar(out=out, in0=in0, scalar1=s1, scalar2=s2,
                                                     op0=op0, op1=op1), reads, writes)

    def STT(self, out, in0, scalar, in1, op0, op1, reads, writes):
        self.S.op("dve", lambda e: e.scalar_tensor_tensor(out=out, in0=in0, scalar=scalar, in1=in1,
                                                          op0=op0, op1=op1), reads, writes)

    def CP(self, eng, out, in_, reads, writes):
        self.S.op(eng, lambda e: e.tensor_copy(out=out, in_=in_), reads, writes)

    def MS(self, eng, ap, val, reads, writes):
        self.S.op(eng, lambda e: e.memset(ap, val), reads, writes)

    def load_weight(self, dst, src, C, N, scale, stage, Rstage, Rscale, ctr, c0=0):
        PIECE = 2048
        for c in range(c0, C):
            for n0 in range(0, N, PIECE):
                pn = min(PIECE, N - n0)
                s = ctr[0] % len(stage)
                ctr[0] += 1
                st = stage[s][:, 0:pn]
                self.S.dma("sp" if s % 2 == 0 else "pool", st, src[c * 128:(c + 1) * 128, n0:n0 + pn], Rstage[s],
                           writes=[Rstage[s]])
                o = dst[:, c, n0:n0 + pn]
                if s % 2 == 0:
                    if scale is None:
                        self.ACT(o, st, AF.Copy, [Rstage[s]], [])
                    else:
                        self.ACT(o, st, AF.Copy, [Rstage[s], Rscale], [], scale=scale[:, c:c + 1])
                else:
                    if scale is None:
                        self.CP("dve", o, st, [Rstage[s]], [])
                    else:
                        self.TS("dve", o, st, scale[:, c:c + 1], None, ALU.mult, None,
                                [Rstage[s], Rscale], [])

    def norm_xn(self, hin, Rh, xn, Rxn, st, Rst):
        nh = self.neghalf
        self.ACT(xn, hin, AF.Square, [Rh], [Rxn, Rst], accum_out=st[:, 0:1])
        self.TS("pool", st[:, 1:2], st[:, 0:1], 1.0 / D, EPS, ALU.mult, ALU.add, [Rst], [Rst])
        self.TT("pool", st[:, 2:3], st[:, 1:2], nh[:, 0:1], ALU.pow, [Rst], [Rst])
        self.ACT(xn, hin, AF.Copy, [Rh, Rst], [Rxn], scale=st[:, 2:3])

    def transposes8(self, src, Rsrc, bank, Rbank):
        for k in range(8):
            self.TR(self.psb[:, bank * 1024 + k * 128: bank * 1024 + (k + 1) * 128],
                    src[:, k * 128:(k + 1) * 128], [Rsrc], [Rbank], signal=(k == 7))

    def bank_bf_view(self, bank):
        return self.psb[:, bank * 1024:(bank + 1) * 1024].rearrange("p (k t) -> p k t", k=8, t=128)

    def phase_gmlp(self, hin_d, hout_d, Rin, Rout, w, prefetch_ffn=False):
        S = self.S
        S.barrier()
        self.off = self.const_base
        Winp = None
        if prefetch_ffn:
            Winp = self.alloc([8, 2 * DFF], BF16)
        Wu = self.alloc([8, 1024], BF16)
        Wv = self.alloc([8, 1024], BF16)
        Wo = self.alloc([8, 1024], BF16)
        wsT = self.alloc([8, 128], BF16)
        bs128 = self.alloc([8, 128], BF16)
        ones128 = self.alloc([128], BF16)
        gvb = self.alloc([1024], F32)
        gcol = self.alloc([8], F32)
        gcolF = self.alloc([8], F32)
        mark = self.off
        stage = [self.alloc([2048], F32) for _ in range(4)]
        wsf = self.alloc([8, 128], F32)
        wsb = self.alloc([8, 128], BF16)
        btmp = self.alloc([1024], F32)
        bhi = self.alloc([1024], BF16)
        blo = self.alloc([1024], BF16)

        Rstage = [Res("stage%d" % i) for i in range(4)]
        Rs = Res("setup")
        Rg = Res("gcol")
        ctr = [0]
        S.dma("sp", gcol, w["a_gcol"][:, :], Rg, writes=[Rg])
        if prefetch_ffn:
            RgF = Res("gcolF")
            S.dma("sp", gcolF, w["f_gcol0"][:, :], RgF, writes=[RgF])
        self.load_weight(Wu, w["a_w_in"][:, 0:1024], 8, 1024, gcol, stage, Rstage, Rg, ctr)
        self.load_weight(Wv, w["a_w_in"][:, 1024:2048], 8, 1024, gcol, stage, Rstage, Rg, ctr)
        self.load_weight(Wo, w["a_w_out"], 8, 1024, None, stage, Rstage, Rg, ctr)
        Rgv = Res("gvb")
        S.dma("sp", gvb, w["a_v_norm_g"].partition_broadcast(128), Rgv, writes=[Rgv])
        Rws = Res("ws")
        S.dma("sp", wsf, w["a_w_s"].rearrange("g i j -> i g j"), Rws, writes=[Rws])
        self.CP("dve", wsb, wsf, [Rws], [Rws])
        Rb0 = Res("b0")
        for g in range(8):
            self.TR(self.psb[:, g * 128:(g + 1) * 128], wsb[:, g, :], [Rws], [Rb0], signal=(g == 7))
        self.CP("dve", wsT, self.bank_bf_view(0), [Rb0], [Rs])
        self.MS("dve", wsT[64:128, :, 0:64], 0.0, [Rs], [Rs])
        Rb = Res("bs")
        self.MS("pool", btmp, 0.0, [], [Rb])
        S.dma("sp", btmp[0:1, :], w["a_b_s"][0:1, :], Rb, reads=[], writes=[Rb])
        Rb2 = Res("bs2")
        S.dma("sp", btmp[32:33, :], w["a_b_s"][0:1, :], Rb2, reads=[Rb], writes=[Rb])
        self.CP("dve", bhi[0:64, :], btmp[0:64, :], [Rb], [Rb])
        self.TT("dve", blo[0:64, :], btmp[0:64, :], bhi[0:64, :], ALU.subtract, [Rb], [Rb])
        bsf = bs128.rearrange("p a b -> p (a b)")
        self.MS("dve", bsf, 0.0, [Rb], [Rb])
        self.CP("dve", bsf[0:1, :], bhi[0:1, :], [Rb], [Rb])
        self.CP("dve", bsf[32:33, :], blo[32:33, :], [Rb], [Rb])
        self.MS("dve", ones128, 0.0, [Rb], [Rb])
        self.MS("dve", ones128[0:1, :], 1.0, [Rb], [Rb])
        self.MS("dve", ones128[32:33, :], 1.0, [Rb], [Rb])
        S.barrier()
        self.off = mark
        hin = [self.alloc([1024], F32) for _ in range(4)]
        xn = [self.alloc([1024], BF16) for _ in range(2)]
        hnT = [self.alloc([8, 128], BF16) for _ in range(2)]
        u_sb = [self.alloc([1024], F32) for _ in range(2)]
        v_sb = [self.alloc([1024], F32) for _ in range(2)]
        vn = [self.alloc([1024], BF16) for _ in range(2)]
        prod = [self.alloc([8, 128], BF16) for _ in range(2)]
        st = [self.alloc([4], F32) for _ in range(2)]
        stv = [self.alloc([4], F32) for _ in range(2)]
        pstage = [self.alloc([1408], F32) for _ in range(2)] if prefetch_ffn else None
        Rps = [Res("ps%d" % i) for i in range(2)]
        pieces = [(c, n0) for c in range(8) for n0 in range(0, 2 * DFF, 1408)]

        def pf_dma(n):
            c, n0 = pieces[n]
            S.dma("sp", pstage[n % 2], w["f_w_in0"][c * 128:(c + 1) * 128, n0:n0 + 1408], Rps[n % 2],
                  writes=[Rps[n % 2]])

        def pf_cast(n):
            c, n0 = pieces[n]
            self.ACT(Winp[:, c, n0:n0 + 1408], pstage[n % 2], AF.Copy, [Rps[n % 2]], [], scale=gcolF[:, c:c + 1])

        NT = NTOK // 128
        Rh = [Res("hin%d" % i) for i in range(4)]
        Rsto = [Res() for _ in range(4)]
        Rxn = [Res() for _ in range(2)]
        Rst = [Res() for _ in range(2)]
        Rstv = [Res() for _ in range(2)]
        RhnT = [Res() for _ in range(2)]
        Ru = [Res() for _ in range(2)]
        Rv = [Res() for _ in range(2)]
        Rvn = [Res() for _ in range(2)]
        Rpr = [Res() for _ in range(2)]
        Rho = [Res() for _ in range(2)]
        RT = [Res("T0"), Res("T7")]
        Tbank = [0, 7]
        RpsU2, RpsV, RpsS = [Res("psU0"), Res("psU1")], Res("psV"), Res("psS")
        psU = self.ps[:, 512:1536]
        psV = self.ps[:, 1536:2560]
        psS = self.ps[:, 2560:3584]

        def load(i):
            s3 = i % 4
            S.dma("sp", hin[s3], hin_d[i * 128:(i + 1) * 128, :], Rh[s3], reads=[Rin[i]], writes=[Rh[s3]])

        def norm0(i):
            s, s3 = i % 2, i % 4
            self.ACT(xn[s], hin[s3], AF.Square, [Rh[s3]], [Rxn[s], Rst[s]], accum_out=st[s][:, 0:1])
            self.TS("pool", st[s][:, 1:2], st[s][:, 0:1], 1.0 / D, EPS, ALU.mult, ALU.add, [Rst[s]], [Rst[s]])
            self.TT("pool", st[s][:, 2:3], st[s][:, 1:2], self.neghalf[:, 0:1], ALU.pow, [Rst[s]], [Rst[s]])

        def norm1(i):
            s, s3 = i % 2, i % 4
            self.ACT(xn[s], hin[s3], AF.Copy, [Rh[s3], Rst[s]], [Rxn[s]], scale=st[s][:, 2:3])

        def tr(i):
            s = i % 2
            self.transposes8(xn[s], Rxn[s], Tbank[s], RT[s])
            self.CP("dve", hnT[s], self.bank_bf_view(Tbank[s]), [RT[s]], [RhnT[s]])

        def stU(i):
            s = i % 2
            for c in range(8):
                for k in range(8):
                    self.MM(psU[:, c * 128:(c + 1) * 128], Wu[:, k, c * 128:(c + 1) * 128], hnT[s][:, k, :],
                            k == 0, k == 7, [RhnT[s]], [RpsU2[c // 4]], signal=(c % 4 == 3 and k == 7))
            self.ACT(u_sb[s], psU, AF.Gelu_apprx_tanh, RpsU2, [Ru[s]])

        def stV(i):
            s = i % 2
            for half in range(2):
                for k in range(8):
                    self.MM(psV[:, half * 512:(half + 1) * 512], hnT[s][:, k, :], Wv[:, k, half * 512:(half + 1) * 512],
                            k == 0, k == 7, [RhnT[s]], [RpsV], signal=(half == 1 and k == 7))
            self.ACT(v_sb[s], psV, AF.Gelu_apprx_tanh, [RpsV], [Rv[s]])
            self.ACT(vn[s], v_sb[s], AF.Square, [Rv[s]], [Rvn[s], Rstv[s]], accum_out=stv[s][:, 0:1])
            self.TS("pool", stv[s][:, 1:2], stv[s][:, 0:1], 1.0 / D, EPS, ALU.mult, ALU.add, [Rstv[s]], [Rstv[s]])
            self.TT("pool", stv[s][:, 2:3], stv[s][:, 1:2], self.neghalf[:, 0:1], ALU.pow, [Rstv[s]], [Rstv[s]])

        def stVN(i):
            s = i % 2
            self.STT(vn[s], v_sb[s], stv[s][:, 2:3], gvb, ALU.mult, ALU.mult, [Rv[s], Rstv[s]], [Rvn[s]])

        def stS(i):
            s = i % 2
            for g in range(8):
                self.MM(psS[:, g * 128:(g + 1) * 128], vn[s][:, g * 128:(g + 1) * 128], wsT[:, g, :],
                        True, False, [Rvn[s]], [RpsS], signal=False)
                self.MM(psS[:, g * 128:(g + 1) * 128], ones128, bs128[:, g, :],
                        False, True, [], [RpsS], signal=(g == 7))
            self.TT("dve", prod[s].rearrange("p a b -> p (a b)"), psS, u_sb[s], ALU.mult, [RpsS, Ru[s]], [Rpr[s]])

        def stO(i):
            s, s3 = i % 2, i % 4
            for half in range(2):
                for k in range(8):
                    self.MM(psU[:, half * 512:(half + 1) * 512], prod[s][:, k, :], Wo[:, k, half * 512:(half + 1) * 512],
                            k == 0, k == 7, [Rpr[s]], [RpsU2[half]], signal=(k == 7))
                self.TT("dve", hin[s3][:, half * 512:(half + 1) * 512], psU[:, half * 512:(half + 1) * 512],
                        hin[s3][:, half * 512:(half + 1) * 512], ALU.add, [RpsU2[half], Rh[s3]], [Rh[s3]])
            S.dma("pool", hout_d[i * 128:(i + 1) * 128, :], hin[s3], Rsto[s3], reads=[Rh[s3]], writes=[Rout[i]])

        load(0)
        load(1)
        load(2)
        for i0 in range(2):
            norm0(i0)
            norm1(i0)
            tr(i0)
        stU(0)
        stV(0)
        stVN(0)
        if prefetch_ffn:
            pf_dma(0)
            pf_dma(1)
        for i in range(NT):
            if prefetch_ffn and i < len(pieces):
                pf_cast(i)
                if i + 2 < len(pieces):
                    pf_dma(i + 2)
            if i + 3 < NT:
                load(i + 3)
            if i + 2 < NT:
                norm0(i + 2)
            if i + 1 < NT:
                stU(i + 1)
            if i + 2 < NT:
                norm1(i + 2)
            stS(i)
            if i + 1 < NT:
                stV(i + 1)
            if i + 2 < NT:
                tr(i + 2)
            stO(i)
            if i + 1 < NT:
                stVN(i + 1)

    def phase_ffn(self, hin_d, hout_d, Rin, Rout, w, l, final, win_preloaded=False, win_c0=0):
        S = self.S
        S.barrier()
        self.off = self.const_base
        Win = self.alloc([8, 2 * DFF], BF16)
        Wd = self.alloc([22, 1024], BF16)
        cw = self.alloc([44, 4], F32)
        gcol = self.alloc([8], F32)
        gfb = self.alloc([1024], F32) if final else None
        mark = self.off
        stage = [self.alloc([2048], F32) for _ in range(NSTAGE)]
        Rstage = [Res("stage%d" % i) for i in range(NSTAGE)]
        Rg = Res("gcol")
        ctr = [0]
        S.dma("sp", gcol, w["f_gcol%d" % l][:, :], Rg, writes=[Rg])
        Rc = Res("cw")
        S.dma("sp", cw.rearrange("p a b -> p (a b)"), w["f_cw%d" % l][:, :], Rc, writes=[Rc])
        if final:
            Rgf = Res("gfb")
            S.dma("sp", gfb, w["final_g"].partition_broadcast(128), Rgf, writes=[Rgf])
        if not win_preloaded:
            self.load_weight(Win, w["f_w_in%d" % l], 8, 2 * DFF, gcol, stage, Rstage, Rg, ctr, c0=win_c0)
        S.barrier()
        self.off = mark
        TT_ = 256
        NTL = NTOK // TT_
        hin = [[self.alloc([1024], F32) for _ in range(2)] for _ in range(3)]
        xn = [self.alloc([1024], BF16) for _ in range(2)]
        st = [self.alloc([4], F32) for _ in range(4)]
        hnT = [self.alloc([8, TT_ + 2], BF16) for _ in range(2)]
        accU = [self.alloc([TT_], F32) for _ in range(3)]
        accG = [self.alloc([TT_], F32) for _ in range(3)]
        gated = [self.alloc([22, TT_], BF16) for _ in range(2)]
        junk = self.alloc([1024], BF16)

        Rh = [[Res() for _ in range(2)] for _ in range(3)]
        Rsto = [[Res() for _ in range(2)] for _ in range(3)]
        Rxn = [Res() for _ in range(2)]
        Rst = [Res() for _ in range(4)]
        RhnT = [Res() for _ in range(2)]
        RaU = [Res() for _ in range(3)]
        RaG = [Res() for _ in range(3)]
        RpU = [Res() for _ in range(3)]
        RpG = [Res() for _ in range(3)]
        Rga = [[Res() for _ in range(22)] for _ in range(2)]
        Rmisc = [Res("m6"), Res("m7")]
        misc_bank = [6, 7]
        mctr = [0]
        Rjunk = Res("junk")
        NW = TT_ + 2

        def front_load(tt):
            for sub in range(2):
                t0 = tt * TT_ + sub * 128
                S.dma("sp", hin[tt % 3][sub], hin_d[t0:t0 + 128, :], Rh[tt % 3][sub], reads=[Rin[t0 // 128]],
                      writes=[Rh[tt % 3][sub]])

        def front_norm(tt, sub, part):
            si = (tt % 2) * 2 + sub
            h, Rhh = hin[tt % 3][sub], Rh[tt % 3][sub]
            if part == 0:
                self.ACT(xn[sub], h, AF.Square, [Rhh], [Rxn[sub], Rst[si]], accum_out=st[si][:, 0:1])
                self.TS("pool", st[si][:, 1:2], st[si][:, 0:1], 1.0 / D, EPS, ALU.mult, ALU.add, [Rst[si]], [Rst[si]])
                self.TT("pool", st[si][:, 2:3], st[si][:, 1:2], self.neghalf[:, 0:1], ALU.pow, [Rst[si]], [Rst[si]])
            else:
                self.ACT(xn[sub], h, AF.Copy, [Rhh, Rst[si]], [Rxn[sub]], scale=st[si][:, 2:3])

        def front_T(tt, sub):
            par = tt % 2
            m = mctr[0] % 2
            mctr[0] += 1
            self.transposes8(xn[sub], Rxn[sub], misc_bank[m], Rmisc[m])
            self.CP("dve", hnT[par][:, :, 2 + sub * 128: 2 + (sub + 1) * 128], self.bank_bf_view(misc_bank[m]),
                    [Rmisc[m]], [RhnT[par]])
            if sub == 0:
                if tt % (SEQ // TT_) == 0:
                    self.MS("pool", hnT[par][:, :, 0:2], 0.0, [], [RhnT[par]])
                else:
                    self.CP("pool", hnT[par][:, :, 0:2], hnT[1 - par][:, :, TT_:TT_ + 2], [RhnT[1 - par]], [RhnT[par]])

        def up_pe_evac_stt(tt, j):
            par = tt % 2
            b = j % 3
            psU = self.ps[:, (2 * b) * 512:(2 * b) * 512 + NW]
            psG = self.ps[:, (2 * b + 1) * 512:(2 * b + 1) * 512 + NW]
            for (pst, Rp, c) in ((psU, RpU[b], j), (psG, RpG[b], 22 + j)):
                for k in range(8):
                    self.MM(pst, Win[:, k, c * 128:(c + 1) * 128], hnT[par][:, k, :], k == 0, k == 7,
                            [RhnT[par]], [Rp], signal=(k == 7))
            for (pst, Rp, c, acc, Ra) in ((psU, RpU[b], j, accU[b], RaU[b]), (psG, RpG[b], 22 + j, accG[b], RaG[b])):
                self.ACT(acc, pst[:, 2:NW], AF.Identity, [Rp], [Ra], scale=cw[:, c, 2:3], bias=cw[:, c, 3:4])
            for (pst, Rp, c, acc, Ra) in ((psU, RpU[b], j, accU[b], RaU[b]), (psG, RpG[b], 22 + j, accG[b], RaG[b])):
                self.STT(acc, pst[:, 1:NW - 1], cw[:, c, 1:2], acc, ALU.mult, ALU.add, [Rp, Ra], [Ra])
                self.STT(acc, pst[:, 0:NW - 2], cw[:, c, 0:1], acc, ALU.mult, ALU.add, [Rp, Ra], [Ra])

        def up_silu(tt, j):
            b = j % 3
            psG = self.ps[:, (2 * b + 1) * 512:(2 * b + 1) * 512 + NW]
            self.ACT(accG[b], accG[b], AF.Silu, [RaG[b]], [RaG[b]])

        def up_prod(tt, j):
            par = tt % 2
            b = j % 3
            psG = self.ps[:, (2 * b + 1) * 512:(2 * b + 1) * 512 + NW]
            self.TT("pool", gated[par][:, j, :], accG[b], accU[b], ALU.mult, [RaG[b], RaU[b]], [Rga[par][j]])

        def down_group(tt, sub, half):
            par = tt % 2
            t0 = tt * TT_ + sub * 128
            ti = t0 // 128
            h = hin[tt % 3][sub]
            Rhh = Rh[tt % 3][sub]
            m = mctr[0] % 2
            mctr[0] += 1
            psD = self.bank(misc_bank[m])
            for k in range(22):
                self.MM(psD, gated[par][:, k, sub * 128:(sub + 1) * 128], Wd[:, k, half * 512:(half + 1) * 512],
                        k == 0, k == 21, [Rga[par][k], RWd[k]], [Rmisc[m]], signal=(k == 21))
            self.TT("dve", h[:, half * 512:(half + 1) * 512], psD, h[:, half * 512:(half + 1) * 512], ALU.add,
                    [Rmisc[m], Rhh], [Rhh])
            if half == 1:
                if final:
                    si = par * 2 + sub
                    self.ACT(junk, h, AF.Square, [Rhh], [Rjunk, Rst[si]], accum_out=st[si][:, 0:1])
                    self.TS("pool", st[si][:, 1:2], st[si][:, 0:1], 1.0 / D, EPS, ALU.mult, ALU.add, [Rst[si]], [Rst[si]])
                    self.TT("pool", st[si][:, 2:3], st[si][:, 1:2], self.neghalf[:, 0:1], ALU.pow, [Rst[si]], [Rst[si]])
                    self.STT(h, h, st[si][:, 2:3], gfb, ALU.mult, ALU.mult, [Rhh, Rst[si]], [Rhh])
                S.dma("pool", hout_d[t0:t0 + 128, :], h, Rsto[tt % 3][sub], reads=[Rhh], writes=[Rout[ti]])

        RWd = [Res("wd%d" % k) for k in range(22)]

        def wd_dma(k):
            S.dma("sp", hin[2][k % 2], w["f_w_down%d" % l][k * 128:(k + 1) * 128, :], Rh[2][k % 2],
                  writes=[Rh[2][k % 2]])

        def wd_cast(k):
            self.ACT(Wd[:, k, :], hin[2][k % 2], AF.Copy, [Rh[2][k % 2]], [RWd[k]])

        DOWN_AT = {2: (0, 0), 6: (0, 1), 10: (1, 0), 14: (1, 1)}
        wd_dma(0)
        wd_dma(1)
        NORM_AT = {1: (0, 0), 4: (0, 1), 9: (1, 0), 12: (1, 1)}
        FRONT_AT = {8: 0, 16: 1}
        front_load(0)
        for sub in range(2):
            front_norm(0, sub, 0)
            front_norm(0, sub, 1)
            front_T(0, sub)
        for tt in range(NTL + 1):
            for j in range(22):
                if j == 0 and tt + 1 < NTL:
                    front_load(tt + 1)
                if tt < NTL:
                    up_pe_evac_stt(tt, j)
                    if j >= 1:
                        up_silu(tt, j - 1)
                        up_prod(tt, j - 1)
                if tt == 0:
                    wd_cast(j)
                    if j + 2 < 22:
                        wd_dma(j + 2)
                if tt >= 1 and j in DOWN_AT:
                    down_group(tt - 1, *DOWN_AT[j])
                if j in NORM_AT and tt + 1 < NTL:
                    front_norm(tt + 1, *NORM_AT[j])
                if j in FRONT_AT and tt + 1 < NTL:
                    front_T(tt + 1, FRONT_AT[j])
            if tt < NTL:
                up_silu(tt, 21)
                up_prod(tt, 21)

    def phase_attn(self, hin_d, hout_d, Rin, Rout, w, prefetch_ffn=False):
        S = self.S
        S.barrier()
        self.off = self.const_base
        Winp = self.alloc([8, 2 * DFF], BF16)
        self.off = self.const_base
        Wk = self.alloc([8, 1024], BF16)
        Wv = self.alloc([8, 1024], BF16)
        Wq = self.alloc([8, 1024], BF16)
        Wo = self.alloc([8, 1024], BF16)
        KT = self.alloc([8, SEQ], BF16)
        V = self.alloc([16, 16, 66], BF16)
        E = self.alloc([16, 3, 128], BF16)
        cb = self.alloc([16], F32)
        ncb = self.alloc([16], F32)
        gkv = self.alloc([8], F32)
        gq = self.alloc([8], F32)
        gcolF = self.alloc([8], F32)
        Qz = [self.alloc([8, 2, 128], BF16) for _ in range(2)]
        mark = self.off
        stage = [self.alloc([2048], F32) for _ in range(4)]
        btile = self.alloc([16, 2, 128], F32)
        Rstage = [Res("stage%d" % i) for i in range(4)]
        Rg = Res("g")
        ctr = [0]
        Re = Res("E")
        Rz = Res("qz")
        self.MS("pool", V.rearrange("p a b c -> p (a b c)"), 1.0, [], [Re])
        for q in Qz:
            self.MS("pool", q.rearrange("p a b c -> p (a b c)"), 0.0, [], [Rz])
        S.dma("sp", gkv, w["kv_gcol"][:, :], Rg, writes=[Rg])
        if prefetch_ffn:
            RgF = Res("gcolF")
            S.dma("sp", gcolF, w["f_gcol1"][:, :], RgF, writes=[RgF])
        Rg2 = Res("g2")
        S.dma("sp", gq, w["b_gcol"][:, :], Rg2, writes=[Rg2])
        Rbt = Res("bt")
        S.dma("sp", btile.rearrange("p a b c -> p (a b c)"), w["btile"][:, :], Rbt, writes=[Rbt])
        Rcb = Res("cb")
        S.dma("sp", cb, w["cbias"][:, :], Rcb, writes=[Rcb])
        self.TS("dve", gq, gq, 0.125, None, ALU.mult, None, [Rg2], [Rg2])
        self.TS("dve", ncb, cb, -1.0, None, ALU.mult, None, [Rcb], [Rcb])
        Re2 = Res("E2")
        for h in range(16):
            self.ACT(E[:, h, 0:2, :], btile[:, h, :, :], AF.Exp, [Rbt, Rcb], [Re2], bias=ncb[:, h:h + 1])
        self.MS("dve", E[:, :, 2, :], 1.0, [Re2], [Re2])
        self.MS("dve", E[64:128, :, 0, 0:64], 0.0, [Re2], [Re2])
        self.MS("dve", E[0:64, :, 2, 64:128], 0.0, [Re2], [Re2])
        self.load_weight(Wk, w["w_kv"][:, 0:1024], 8, 1024, gkv, stage, Rstage, Rg, ctr)
        self.load_weight(Wv, w["w_kv"][:, 1024:2048], 8, 1024, gkv, stage, Rstage, Rg, ctr)
        self.load_weight(Wq, w["b_w_q"], 8, 1024, gq, stage, Rstage, Rg2, ctr)
        self.load_weight(Wo, w["b_w_o"], 8, 1024, None, stage, Rstage, Rg, ctr)
        S.barrier()
        self.off = mark
        hin = [self.alloc([1024], F32) for _ in range(3)]
        xn = [self.alloc([1024], BF16) for _ in range(2)]
        st = [self.alloc([4], F32) for _ in range(2)]
        hnT = [self.alloc([8, 128], BF16) for _ in range(2)]
        PT = [self.alloc([640], BF16) for _ in range(4)]
        osb = [self.alloc([132], F32) for _ in range(2)]
        O_sb = [self.alloc([1024], BF16) for _ in range(2)]
        OT = [self.alloc([8, 128], BF16) for _ in range(2)]
        rc = [self.alloc([2], F32) for _ in range(2)]
        pstage = [self.alloc([1408], F32) for _ in range(2)]
        Rps = [Res("ps0"), Res("ps1")]
        RWdead = Res("wkvq")
        pieces = [(c, n0) for c in range(4) for n0 in range(0, 2 * DFF, 1408)]

        def pf_dma(n):
            c, n0 = pieces[n]
            S.dma("sp", pstage[n % 2], w["f_w_in1"][c * 128:(c + 1) * 128, n0:n0 + 1408], Rps[n % 2],
                  writes=[Rps[n % 2]])

        def pf_cast(n):
            c, n0 = pieces[n]
            self.ACT(Winp[:, c, n0:n0 + 1408], pstage[n % 2], AF.Copy, [Rps[n % 2]], [RWdead], scale=gcolF[:, c:c + 1])

        Rh = [Res() for _ in range(3)]
        Rxn = [Res() for _ in range(2)]
        Rst = [Res() for _ in range(2)]
        RhnT = [Res() for _ in range(2)]
        RQz = [Res() for _ in range(2)]
        RPT = [Res() for _ in range(4)]
        Rosb = [Res(), Res()]
        ROs = [Res() for _ in range(2)]
        ROT = [Res() for _ in range(2)]
        Rrc = [Res() for _ in range(2)]
        RKT = [Res() for _ in range(16)]
        RV = [Res() for _ in range(16)]
        Rsto = [Res(), Res(), Res()]
        RpS = [Res("pS0"), Res("pS1"), Res("pS2")]
        RpO = [Res("pO0"), Res("pO1")]
        Rmisc = RpS
        misc_bank = [0, 2, 4]
        rot = [0]
        mctr = [0]
        pctr = [0]
        evctr = [0]
        POS = {0: 0, 1: 1, 4: 2, 2: 3, 3: 4}

        def nextm():
            b = rot[0] % 3
            rot[0] += 1
            return b

        def evac(out, in_, reads, writes, bump=True):
            if bump:
                evctr[0] += 1
            if evctr[0] % 2 == 0:
                self.ACT(out, in_, AF.Copy, reads, writes)
            else:
                self.CP("dve", out, in_, reads, writes)

        def u_load(gi):
            s3 = gi % 3
            S.dma("sp", hin[s3], hin_d[gi * 128:(gi + 1) * 128, :], Rh[s3], reads=[Rin[gi]], writes=[Rh[s3]])

        def u_norm0(gi):
            s = gi % 2
            self.ACT(xn[s], hin[gi % 3], AF.Square, [Rh[gi % 3]], [Rxn[s], Rst[s]], accum_out=st[s][:, 0:1])
            self.TS("pool", st[s][:, 1:2], st[s][:, 0:1], 1.0 / D, EPS, ALU.mult, ALU.add, [Rst[s]], [Rst[s]])
            self.TT("pool", st[s][:, 2:3], st[s][:, 1:2], self.neghalf[:, 0:1], ALU.pow, [Rst[s]], [Rst[s]])

        def u_norm1(gi):
            s = gi % 2
            self.ACT(xn[s], hin[gi % 3], AF.Copy, [Rh[gi % 3], Rst[s]], [Rxn[s]], scale=st[s][:, 2:3])

        def u_T(gi):
            s = gi % 2
            m = nextm()
            self.transposes8(xn[s], Rxn[s], misc_bank[m], Rmisc[m])
            self.CP("dve", hnT[s], self.bank_bf_view(misc_bank[m]), [Rmisc[m]], [RhnT[s]])

        def u_K(gi, hb):
            s = gi % 2
            i = gi % 16
            m = nextm()
            pm = self.bank(misc_bank[m])
            for prl in range(4):
                pr = hb * 4 + prl
                for k in range(8):
                    self.MM(pm[:, prl * 128:(prl + 1) * 128], Wk[:, k, pr * 128:(pr + 1) * 128], hnT[s][:, k, :],
                            k == 0, k == 7, [RhnT[s], RWdead], [Rmisc[m]], signal=(prl == 3 and k == 7))
            evac(KT[:, hb * 4:(hb + 1) * 4, i * 128:(i + 1) * 128], pm.rearrange("p (a b) -> p a b", a=4, b=128),
                 [Rmisc[m]], [RKT[i]])

        def u_V(gi, half):
            s = gi % 2
            i = gi % 16
            m = nextm()
            pm = self.bank(misc_bank[m])
            for k in range(8):
                self.MM(pm, hnT[s][:, k, :], Wv[:, k, half * 512:(half + 1) * 512], k == 0, k == 7,
                        [RhnT[s], RWdead], [Rmisc[m]], signal=(k == 7))
            evac(V[:, i, half * 8:(half + 1) * 8, 0:64], pm.rearrange("p (a b) -> p a b", a=8, b=64),
                 [Rmisc[m]], [RV[i]])

        def u_Q(gi, hb):
            s = gi % 2
            m = nextm()
            pm = self.bank(misc_bank[m])
            for prl in range(4):
                pr = hb * 4 + prl
                for k in range(8):
                    self.MM(pm[:, prl * 128:(prl + 1) * 128], Wq[:, k, pr * 128:(pr + 1) * 128], hnT[s][:, k, :],
                            k == 0, k == 7, [RhnT[s], RWdead], [Rmisc[m]], signal=(prl == 3 and k == 7))
            pv = pm.rearrange("p (a b) -> p a b", a=4, b=128)
            evac(Qz[s][0:64, hb * 4:(hb + 1) * 4, 0, :], pv[0:64], [Rmisc[m]], [RQz[s]])
            evac(Qz[s][64:128, hb * 4:(hb + 1) * 4, 1, :], pv[64:128], [Rmisc[m]], [RQz[s]], bump=False)

        def bg_units(gi):
            return {-1: [lambda: u_load(gi)], 0: [lambda: u_norm0(gi)],
                    2: [lambda: u_norm1(gi)],
                    4: [lambda: u_T(gi)],
                    6: [lambda: u_K(gi, 0)], 8: [lambda: u_K(gi, 1)],
                    9: [lambda: u_Q(gi, 0)], 11: [lambda: u_Q(gi, 1)],
                    13: [lambda: u_V(gi, 0)], 14: [lambda: u_V(gi, 1)]}

        def attn(gi, bg):
            s = gi % 2
            i = gi % 16
            omax = min(i, 4)
            valid = list(range(omax + 1))
            if i == 0:
                runs = [(0, 0, 0, 128)]
            elif i == 1:
                runs = [(0, 0, 0, 256)]
            elif i == 2:
                runs = [(0, 0, 0, 256), (0, 384, 384, 128)]
            elif i == 3:
                runs = [(0, 0, 0, 256), (0, 384, 384, 128), (1, 0, 512, 128)]
            else:
                runs = [(0, 0, 0, 512), (1, 0, 512, 128)]
            nmul = 128 if i == 0 else (256 if i < 4 else 384)
            state = {}

            def s_part(hd):
                pr, e = hd // 2, hd % 2
                sb_ = nextm()
                pS2 = [self.bank(2 * sb_), self.bank(2 * sb_ + 1)]
                for n, o in enumerate(valid):
                    p = POS[o]
                    dst = pS2[0][:, p * 128:(p + 1) * 128] if p < 4 else pS2[1][:, 0:128]
                    self.MM(dst, KT[:, pr, (i - o) * 128:(i - o + 1) * 128], Qz[s][:, pr, e, :],
                            True, True, [RKT[i - o], RQz[s]], [RpS[sb_]], signal=(n == len(valid) - 1))
                pb = pctr[0] % 4
                pctr[0] += 1
                state[hd] = pb
                for (src, a, pa, wd) in runs:
                    self.ACT(PT[pb][:, pa:pa + wd], pS2[src][:, a:a + wd], AF.Exp, [RpS[sb_]], [RPT[pb]],
                             bias=cb[:, hd:hd + 1])

            def m_part(hd):
                pb = state[hd]
                self.TT("dve", PT[pb][:, 0:nmul], PT[pb][:, 0:nmul],
                        E[:, hd, :, :].rearrange("p a b -> p (a b)")[:, 0:nmul], ALU.mult, [RPT[pb]], [RPT[pb]])

            def pv_part(hd):
                pr, e = hd // 2, hd % 2
                ob = pr % 2
                pb = state[hd]
                pO = self.ps[:, (6 + ob) * 512:(6 + ob) * 512 + 132]
                for n, o in enumerate(valid):
                    p = POS[o]
                    self.MM(pO[:, e * 66:e * 66 + 65], PT[pb][:, p * 128:(p + 1) * 128], V[:, i - o, hd, 0:65],
                            n == 0, n == len(valid) - 1, [RPT[pb], RV[i - o]], [RpO[ob]], signal=(n == len(valid) - 1))
                if e == 1:
                    o3 = osb[ob].rearrange("p (a b) -> p a b", a=2, b=66)
                    self.CP("dve", o3[:, :, 0:65], pO.rearrange("p (a b) -> p a b", a=2, b=66)[:, :, 0:65], [RpO[ob]], [Rosb[ob]])
                    self.S.op("dve", lambda e_: e_.reciprocal(out=rc[ob], in_=o3[:, :, 64]), [Rosb[ob]], [Rrc[ob]])
                    for ee in range(2):
                        h2 = pr * 2 + ee
                        self.TS("dve", O_sb[s][:, h2 * 64:(h2 + 1) * 64], osb[ob][:, ee * 66:ee * 66 + 64], rc[ob][:, ee:ee + 1],
                                None, ALU.mult, None, [Rosb[ob], Rrc[ob]], [ROs[s]])

            for f in bg.get(-1, []):
                f()
            s_part(0)
            m_part(0)
            s_part(1)
            m_part(1)
            for hd in range(16):
                if hd + 2 < 16:
                    s_part(hd + 2)
                pv_part(hd)
                if hd + 2 < 16:
                    m_part(hd + 2)
                for f in bg.get(hd, []):
                    f()

        def out_T(gi):
            s = gi % 2
            m = nextm()
            self.transposes8(O_sb[s], ROs[s], misc_bank[m], Rmisc[m])
            self.CP("dve", OT[s], self.bank_bf_view(misc_bank[m]), [Rmisc[m]], [ROT[s]])

        def out_wo(gi, half):
            s = gi % 2
            m = nextm()
            pm = self.bank(misc_bank[m])
            for k in range(8):
                self.MM(pm, OT[s][:, k, :], Wo[:, k, half * 512:(half + 1) * 512], k == 0, k == 7,
                        [ROT[s]], [Rmisc[m]], signal=(k == 7))
            self.TT("dve", hin[gi % 3][:, half * 512:(half + 1) * 512], pm, hin[gi % 3][:, half * 512:(half + 1) * 512], ALU.add,
                    [Rmisc[m], Rh[gi % 3]], [Rh[gi % 3]])
            if half == 1:
                S.dma("pool", hout_d[gi * 128:(gi + 1) * 128, :], hin[gi % 3], Rsto[gi % 3], reads=[Rh[gi % 3]], writes=[Rout[gi]])

        NT = NTOK // 128 if DEBUG_NT is None else DEBUG_NT
        u0 = bg_units(0)
        for key in sorted(u0):
            for f in u0[key]:
                f()
        for gi in range(NT):
            bg = bg_units(gi + 1) if gi + 1 < NT else {}
            if gi >= 1:
                bg.setdefault(0, []).insert(0, lambda g=gi - 1: out_T(g))
                bg.setdefault(1, []).append(lambda g=gi - 1: out_wo(g, 0))
                bg.setdefault(3, []).append(lambda g=gi - 1: out_wo(g, 1))
            if prefetch_ffn and gi == NT - 1 and NT == NTOK // 128:
                bg.setdefault(-1, []).extend([lambda: pf_dma(0), lambda: pf_dma(1)])
                for hd_ in range(16):
                    bg.setdefault(hd_, []).append(lambda n=hd_: pf_cast(n))
                    if hd_ + 2 < 16:
                        bg.setdefault(hd_, []).append(lambda n=hd_ + 2: pf_dma(n))
            attn(gi, bg)
        out_T(NT - 1)
        out_wo(NT - 1, 0)
        out_wo(NT - 1, 1)


PHASE_INPUTS = {
    "A": [("a_gcol", [128, 8]), ("a_w_in", [1024, 2048]), ("a_v_norm_g", [1024]), ("a_w_s", [8, 128, 128]),
          ("a_b_s", [1, 1024]), ("a_w_out", [1024, 1024])],
    "F0": [("f_gcol0", [128, 8]), ("f_cw0", [128, 176]), ("f_w_in0", [1024, 2 * DFF]), ("f_w_down0", [DFF, 1024])],
    "B": [("kv_gcol", [128, 8]), ("b_gcol", [128, 8]), ("w_kv", [1024, 2048]), ("b_w_q", [1024, 1024]),
          ("b_w_o", [1024, 1024]), ("btile", [128, 16 * 2 * 128]), ("cbias", [128, 16])],
    "F1": [("f_gcol1", [128, 8]), ("f_cw1", [128, 176]), ("f_w_in1", [1024, 2 * DFF]), ("f_w_down1", [DFF, 1024]),
           ("final_g", [1024])],
}


def build_program(phases):
    nc = bass.Bass("TRN2", target_bir_lowering=False)
    es = ExitStack()
    with es:
        hin_d = nc.dram_tensor("hin", [NTOK, D], F32, kind="ExternalInput").ap()
        hout_d = nc.dram_tensor("hout", [NTOK, D], F32, kind="ExternalOutput").ap()
        B = Builder(nc, es)
        w = {}
        for ph in phases:
            for name, shape in PHASE_INPUTS[ph]:
                w[name] = B.din(name, shape)
        NT = NTOK // 128
        cur_d, cur_R = hin_d, [Res("in%d" % i) for i in range(NT)]
        for pi, ph in enumerate(phases):
            last = pi == len(phases) - 1
            if last:
                nxt_d = hout_d
            else:
                nxt_d = nc.dram_tensor("hmid%d" % pi, [NTOK, D], F32, kind="Internal").ap()
            nxt_R = [Res("h%d_%d" % (pi, i)) for i in range(NT)]
            pre = (pi + 1 < len(phases) and phases[pi + 1] == "F0")
            if ph == "A":
                B.phase_gmlp(cur_d, nxt_d, cur_R, nxt_R, w, prefetch_ffn=pre)
            elif ph == "F0":
                B.phase_ffn(cur_d, nxt_d, cur_R, nxt_R, w, 0, False, win_preloaded=(pi > 0 and phases[pi - 1] == "A"))
            elif ph == "B":
                B.phase_attn(cur_d, nxt_d, cur_R, nxt_R, w,
                             prefetch_ffn=(pi + 1 < len(phases) and phases[pi + 1] == "F1"))
            elif ph == "F1":
                B.phase_ffn(cur_d, nxt_d, cur_R, nxt_R, w, 1, True,
                            win_c0=(4 if (pi > 0 and phases[pi - 1] == "B") else 0))
            cur_d, cur_R = nxt_d, nxt_R
        B.S.barrier()
        with nc.Block() as block:
            B.S.replay(block)
    return nc


def host_layout(inputs):
    f = lambda a: np.ascontiguousarray(np.asarray(a, dtype=np.float32))
    gcol = lambda g: f(np.asarray(g).reshape(8, 128).T)
    W = {}
    W["a_gcol"] = gcol(inputs["a_norm_g"][0])
    W["a_w_in"] = f(inputs["a_w_in"][0])
    W["a_v_norm_g"] = f(inputs["a_v_norm_g"][0])
    W["a_w_s"] = f(inputs["a_w_s"][0])
    W["a_b_s"] = f(np.asarray(inputs["a_b_s"][0]).reshape(1, 1024))
    W["a_w_out"] = f(inputs["a_w_out"][0])
    for l in range(2):
        W["f_gcol%d" % l] = gcol(inputs["f_norm_g"][l])
        cwj = np.asarray(inputs["f_conv_w"][l])
        cbj = np.asarray(inputs["f_conv_b"][l])[None, :]
        allp = np.concatenate([cwj, cbj], axis=0)
        W["f_cw%d" % l] = f(allp.reshape(4, 44, 128).transpose(2, 1, 0).reshape(128, 176))
        W["f_w_in%d" % l] = f(inputs["f_w_in"][l])
        W["f_w_down%d" % l] = f(inputs["f_w_down"][l])
    W["kv_gcol"] = gcol(inputs["kv_norm_g"])
    W["b_gcol"] = gcol(inputs["b_norm_g"][0])
    W["w_kv"] = f(inputs["w_kv"])
    W["b_w_q"] = f(inputs["b_w_q"][0])
    W["b_w_o"] = f(inputs["b_w_o"][0])
    rb = np.asarray(inputs["b_rel_bias"][0])
    k = np.arange(128)[:, None]
    q = np.arange(128)[None, :]
    idx0 = (q - k) + 128
    idx1 = np.minimum(128 + q - k, 128) + 128
    bt = np.stack([rb[:, idx0], rb[:, idx1]], axis=1)
    W["btile"] = f(bt.transpose(2, 0, 1, 3).reshape(128, 16 * 2 * 128))
    W["cbias"] = f(np.broadcast_to(rb[:, 256][None, :], (128, 16)))
    W["final_g"] = f(inputs["final_norm_g"])
    return W


_PROG_CACHE = {}


def run_phases(phases, h_shards, W):
    key = tuple(phases)
    if key not in _PROG_CACHE:
        _PROG_CACHE[key] = build_program(phases)
    nc = _PROG_CACHE[key]
    names = [n for ph in phases for n, _ in PHASE_INPUTS[ph]]
    in_maps = []
    for c in range(N_CORES):
        m = {"hin": h_shards[c]}
        for n in names:
            m[n] = W[n]
        in_maps.append(m)
    res = run_bass_kernel_spmd(nc, in_maps, core_ids=list(range(N_CORES)))
    return [r["hout"] for r in res.results]


LAUNCH_PLAN = [["A", "F0", "B", "F1"]]


def kernel(**inputs):
    x = np.ascontiguousarray(np.asarray(inputs["x"], dtype=np.float32))
    W = host_layout(inputs)
    h = [np.ascontiguousarray(x[2 * c:2 * c + 2].reshape(NTOK, D)) for c in range(N_CORES)]
    for phases in LAUNCH_PLAN:
        h = run_phases(phases, h, W)
    out = np.stack([hc.reshape(2, SEQ, D) for hc in h], axis=0).reshape(16, SEQ, D)
    return out.astype(np.float32)
```

```python
import numpy as np
from contextlib import ExitStack
import concourse.bass as bass
import concourse.mybir as mybir
from concourse.bass_utils import run_bass_kernel_spmd

F32 = mybir.dt.float32
BF16 = mybir.dt.bfloat16
AF = mybir.ActivationFunctionType
ALU = mybir.AluOpType

ENGS = ["pe", "act", "dve", "pool", "sp"]
D = 1024
NTOK = 4096
SEQ = 2048
DFF = 2816
EPS = 1e-6
SB_BYTES = 212000
N_CORES = 8
DEBUG_NT = None
NSTAGE = 6


class Res:
    __slots__ = ("name", "w", "r", "dsem")

    def __init__(self, name=""):
        self.name = name
        self.w = None
        self.r = {}
        self.dsem = None


class Sched:
    def __init__(self, nc, es):
        self.nc = nc
        self.es = es
        self.q = {e: [] for e in ENGS}
        self.cnt = {e: 0 for e in ENGS}
        self.known = {e: {} for e in ENGS}
        self.sem = {}
        self.dsems = []
        self.pe_pending = []
        for e in ["pe", "act", "dve", "pool"]:
            self.sem[e] = es.enter_context(nc.semaphore("s_" + e))

    def new_dsem(self):
        key = "d%d" % len(self.dsems)
        self.sem[key] = self.es.enter_context(self.nc.semaphore("s_" + key))
        ds = [key, 0]
        self.dsems.append(ds)
        return ds

    def _wait(self, eng, ev):
        key, val = ev
        if self.known[eng].get(key, 0) >= val:
            return
        self.known[eng][key] = val
        self.q[eng].append(("wait", key, val))

    def _deps(self, eng, reads, writes, is_dma):
        for r in reads:
            if r.w is not None:
                if is_dma or r.w[0] != eng or eng != "pe":
                    self._wait(eng, r.w)
        for w in writes:
            if w.w is not None and (is_dma or w.w[0] != eng):
                self._wait(eng, w.w)
            for k, v in w.r.items():
                if is_dma or k != eng:
                    self._wait(eng, (k, v))

    def _record(self, ev, reads, writes):
        for r in reads:
            if r.r.get(ev[0], 0) < ev[1]:
                r.r[ev[0]] = ev[1]
        for w in writes:
            w.w = ev
            w.r = {}

    def op(self, eng, fn, reads=(), writes=(), signal=True):
        if eng != "pe":
            assert not self.pe_pending, "non-PE op inside an unsignaled PE group"
        self._deps(eng, reads, writes, False)
        if eng == "pe" and not signal:
            self.q[eng].append(("ins", fn, None, 0))
            self.pe_pending.append((tuple(reads), tuple(writes)))
            return None
        self.cnt[eng] += 1
        ev = (eng, self.cnt[eng])
        self.q[eng].append(("ins", fn, eng, 1))
        if eng == "pe":
            for rr, ww in self.pe_pending:
                self._record(ev, rr, ww)
            self.pe_pending = []
        self._record(ev, reads, writes)
        return ev

    def dma(self, eng, out, in_, owner, reads=(), writes=(), noncontig=False):
        assert not self.pe_pending
        self._deps(eng, reads, writes, True)
        if owner.dsem is None:
            owner.dsem = self.new_dsem()
        owner.dsem[1] += 16
        ev = (owner.dsem[0], owner.dsem[1])
        if noncontig:
            fn = lambda e: e.dma_start(out=out, in_=in_, allow_slow_non_contiguous=True)
        else:
            fn = lambda e: e.dma_start(out=out, in_=in_)
        self.q[eng].append(("ins", fn, ev[0], 16))
        self._record(ev, reads, writes)
        return ev

    def barrier(self):
        assert not self.pe_pending
        for eng in ENGS:
            for f in ["pe", "act", "dve", "pool"]:
                if f != eng and self.cnt[f] > 0:
                    self._wait(eng, (f, self.cnt[f]))
            for ds in self.dsems:
                if ds[1] > 0:
                    self._wait(eng, (ds[0], ds[1]))

    def replay(self, block):
        engmap = {"pe": block.tensor, "act": block.scalar, "dve": block.vector,
                  "pool": block.gpsimd, "sp": block.sync}
        sem = self.sem
        for name in ENGS:
            items = self.q[name]

            def body(e, items=items):
                for it in items:
                    if it[0] == "wait":
                        e.wait_ge(sem[it[1]], it[2])
                    else:
                        ins = it[1](e)
                        if it[2] is not None:
                            ins.then_inc(sem[it[2]], it[3])
            engmap[name](body)


class Builder:
    def __init__(self, nc, es):
        self.nc = nc
        self.es = es
        self.S = Sched(nc, es)
        self.big = es.enter_context(nc.sbuf_tensor("big", [128, SB_BYTES // 2], BF16))
        self.ps = es.enter_context(nc.psum_tensor("ps", [128, 4096], F32))
        self.psb = self.ps.bitcast(BF16)
        self.off = 0
        self.dram = {}
        self.identf = self.alloc([128], F32)
        self.ident = self.alloc([128], BF16)
        self.neghalf = self.alloc([2], F32)
        R = Res("const")
        S = self.S
        identf, ident, neghalf = self.identf, self.ident, self.neghalf
        S.op("pool", lambda e: e.memset(identf, 0.0), writes=[R])
        S.op("pool", lambda e: e.affine_select(out=identf, in_=identf, pattern=[[-1, 128]],
                                               compare_op=ALU.not_equal, fill=1.0, base=0,
                                               channel_multiplier=1), reads=[R], writes=[R])
        S.op("dve", lambda e: e.tensor_copy(out=ident, in_=identf), reads=[R], writes=[R])
        S.op("pool", lambda e: e.memset(neghalf, -0.5), writes=[R])
        self.const_base = self.off

    def alloc(self, shape, dtype):
        n = int(np.prod(shape))
        nbytes = n * (4 if dtype == F32 else 2)
        start = self.off
        self.off += (nbytes + 63) // 64 * 64
        assert self.off <= SB_BYTES, "SBUF overflow: %d" % self.off
        ap = self.big[:, start // 2: start // 2 + nbytes // 2]
        if dtype == F32:
            ap = ap.bitcast(F32)
        if len(shape) == 2:
            ap = ap.rearrange("p (a b) -> p a b", a=shape[0], b=shape[1])
        elif len(shape) == 3:
            ap = ap.rearrange("p (a b c) -> p a b c", a=shape[0], b=shape[1], c=shape[2])
        return ap

    def din(self, name, shape, dtype=F32):
        t = self.nc.dram_tensor(name, list(shape), dtype, kind="ExternalInput").ap()
        self.dram[name] = t
        return t

    def bank(self, b, n=512):
        return self.ps[:, b * 512: b * 512 + n]

    def ACT(self, out, in_, func, reads, writes, **kw):
        self.S.op("act", lambda e: e.activation(out=out, in_=in_, func=func, **kw), reads, writes)

    def MM(self, out, lhsT, rhs, start, stop, reads, writes, signal):
        self.S.op("pe", lambda e: e.matmul(out, lhsT=lhsT, rhs=rhs, start=start, stop=stop),
                  reads, writes, signal=signal)

    def TR(self, out, in_, reads, writes, signal):
        ident = self.ident
        self.S.op("pe", lambda e: e.transpose(out=out, in_=in_, identity=ident), reads, writes,
                  signal=signal)

    def TT(self, eng, out, in0, in1, op, reads, writes):
        self.S.op(eng, lambda e: e.tensor_tensor(out=out, in0=in0, in1=in1, op=op), reads, writes)

    def TS(self, eng, out, in0, s1, s2, op0, op1, reads, writes):
        if s2 is None:
            self.S.op(eng, lambda e: e.tensor_scalar(out=out, in0=in0, scalar1=s1, scalar2=None,
                                                     op0=op0), reads, writes)
        else:
            self.S.op(eng, lambda e: e.tensor_scalar(out=out, in0=in0, scalar1=s1, scalar2=s2,
                                                     op0=op0, op1=op1), reads, writes)

    def STT(self, out, in0, scalar, in1, op0, op1, reads, writes):
        self.S.op("dve", lambda e: e.scalar_tensor_tensor(out=out, in0=in0, scalar=scalar, in1=in1,
                                                          op0=op0, op1=op1), reads, writes)

    def CP(self, eng, out, in_, reads, writes):
        self.S.op(eng, lambda e: e.tensor_copy(out=out, in_=in_), reads, writes)

    def MS(self, eng, ap, val, reads, writes):
        self.S.op(eng, lambda e: e.memset(ap, val), reads, writes)

    def load_weight(self, dst, src, C, N, scale, stage, Rstage, Rscale, ctr, c0=0):
        PIECE = 2048
        for c in range(c0, C):
            for n0 in range(0, N, PIECE):
                pn = min(PIECE, N - n0)
                s = ctr[0] % len(stage)
                ctr[0] += 1
                st = stage[s][:, 0:pn]
                self.S.dma("sp" if s % 2 == 0 else "pool", st, src[c * 128:(c + 1) * 128, n0:n0 + pn], Rstage[s],
                           writes=[Rstage[s]])
                o = dst[:, c, n0:n0 + pn]
                if s % 2 == 0:
                    if scale is None:
                        self.ACT(o, st, AF.Copy, [Rstage[s]], [])
                    else:
                        self.ACT(o, st, AF.Copy, [Rstage[s], Rscale], [], scale=scale[:, c:c + 1])
                else:
                    if scale is None:
                        self.CP("dve", o, st, [Rstage[s]], [])
                    else:
                        self.TS("dve", o, st, scale[:, c:c + 1], None, ALU.mult, None,
                                [Rstage[s], Rscale], [])

    def norm_xn(self, hin, Rh, xn, Rxn, st, Rst):
        nh = self.neghalf
        self.ACT(xn, hin, AF.Square, [Rh], [Rxn, Rst], accum_out=st[:, 0:1])
        self.TS("pool", st[:, 1:2], st[:, 0:1], 1.0 / D, EPS, ALU.mult, ALU.add, [Rst], [Rst])
        self.TT("pool", st[:, 2:3], st[:, 1:2], nh[:, 0:1], ALU.pow, [Rst], [Rst])
        self.ACT(xn, hin, AF.Copy, [Rh, Rst], [Rxn], scale=st[:, 2:3])

    def transposes8(self, src, Rsrc, bank, Rbank):
        for k in range(8):
            self.TR(self.psb[:, bank * 1024 + k * 128: bank * 1024 + (k + 1) * 128],
                    src[:, k * 128:(k + 1) * 128], [Rsrc], [Rbank], signal=(k == 7))

    def bank_bf_view(self, bank):
        return self.psb[:, bank * 1024:(bank + 1) * 1024].rearrange("p (k t) -> p k t", k=8, t=128)

    def phase_gmlp(self, hin_d, hout_d, Rin, Rout, w, prefetch_ffn=False):
        S = self.S
        S.barrier()
        self.off = self.const_base
        Winp = None
        if prefetch_ffn:
            Winp = self.alloc([8, 2 * DFF], BF16)
        Wu = self.alloc([8, 1024], BF16)
        Wv = self.alloc([8, 1024], BF16)
        Wo = self.alloc([8, 1024], BF16)
        wsT = self.alloc([8, 128], BF16)
        bs128 = self.alloc([8, 128], BF16)
        ones128 = self.alloc([128], BF16)
        gvb = self.alloc([1024], F32)
        gcol = self.alloc([8], F32)
        gcolF = self.alloc([8], F32)
        mark = self.off
        stage = [self.alloc([2048], F32) for _ in range(4)]
        wsf = self.alloc([8, 128], F32)
        wsb = self.alloc([8, 128], BF16)
        btmp = self.alloc([1024], F32)
        bhi = self.alloc([1024], BF16)
        blo = self.alloc([1024], BF16)

        Rstage = [Res("stage%d" % i) for i in range(4)]
        Rs = Res("setup")
        Rg = Res("gcol")
        ctr = [0]
        S.dma("sp", gcol, w["a_gcol"][:, :], Rg, writes=[Rg])
        if prefetch_ffn:
            RgF = Res("gcolF")
            S.dma("sp", gcolF, w["f_gcol0"][:, :], RgF, writes=[RgF])
        self.load_weight(Wu, w["a_w_in"][:, 0:1024], 8, 1024, gcol, stage, Rstage, Rg, ctr)
        self.load_weight(Wv, w["a_w_in"][:, 1024:2048], 8, 1024, gcol, stage, Rstage, Rg, ctr)
        self.load_weight(Wo, w["a_w_out"], 8, 1024, None, stage, Rstage, Rg, ctr)
        Rgv = Res("gvb")
        S.dma("sp", gvb, w["a_v_norm_g"].partition_broadcast(128), Rgv, writes=[Rgv])
        Rws = Res("ws")
        S.dma("sp", wsf, w["a_w_s"].rearrange("g i j -> i g j"), Rws, writes=[Rws])
        self.CP("dve", wsb, wsf, [Rws], [Rws])
        Rb0 = Res("b0")
        for g in range(8):
            self.TR(self.psb[:, g * 128:(g + 1) * 128], wsb[:, g, :], [Rws], [Rb0], signal=(g == 7))
        self.CP("dve", wsT, self.bank_bf_view(0), [Rb0], [Rs])
        self.MS("dve", wsT[64:128, :, 0:64], 0.0, [Rs], [Rs])
        Rb = Res("bs")
        self.MS("pool", btmp, 0.0, [], [Rb])
        S.dma("sp", btmp[0:1, :], w["a_b_s"][0:1, :], Rb, reads=[], writes=[Rb])
        Rb2 = Res("bs2")
        S.dma("sp", btmp[32:33, :], w["a_b_s"][0:1, :], Rb2, reads=[Rb], writes=[Rb])
        self.CP("dve", bhi[0:64, :], btmp[0:64, :], [Rb], [Rb])
        self.TT("dve", blo[0:64, :], btmp[0:64, :], bhi[0:64, :], ALU.subtract, [Rb], [Rb])
        bsf = bs128.rearrange("p a b -> p (a b)")
        self.MS("dve", bsf, 0.0, [Rb], [Rb])
        self.CP("dve", bsf[0:1, :], bhi[0:1, :], [Rb], [Rb])
        self.CP("dve", bsf[32:33, :], blo[32:33, :], [Rb], [Rb])
        self.MS("dve", ones128, 0.0, [Rb], [Rb])
        self.MS("dve", ones128[0:1, :], 1.0, [Rb], [Rb])
        self.MS("dve", ones128[32:33, :], 1.0, [Rb], [Rb])
        S.barrier()
        self.off = mark
        hin = [self.alloc([1024], F32) for _ in range(4)]
        xn = [self.alloc([1024], BF16) for _ in range(2)]
        hnT = [self.alloc([8, 128], BF16) for _ in range(2)]
        u_sb = [self.alloc([1024], F32) for _ in range(2)]
        v_sb = [self.alloc([1024], F32) for _ in range(2)]
        vn = [self.alloc([1024], BF16) for _ in range(2)]
        prod = [self.alloc([8, 128], BF16) for _ in range(2)]
        st = [self.alloc([4], F32) for _ in range(2)]
        stv = [self.alloc([4], F32) for _ in range(2)]
        pstage = [self.alloc([1408], F32) for _ in range(2)] if prefetch_ffn else None
        Rps = [Res("ps%d" % i) for i in range(2)]
        pieces = [(c, n0) for c in range(8) for n0 in range(0, 2 * DFF, 1408)]

        def pf_dma(n):
            c, n0 = pieces[n]
            S.dma("sp", pstage[n % 2], w["f_w_in0"][c * 128:(c + 1) * 128, n0:n0 + 1408], Rps[n % 2],
                  writes=[Rps[n % 2]])

        def pf_cast(n):
            c, n0 = pieces[n]
            self.ACT(Winp[:, c, n0:n0 + 1408], pstage[n % 2], AF.Copy, [Rps[n % 2]], [], scale=gcolF[:, c:c + 1])

        NT = NTOK // 128
        Rh = [Res("hin%d" % i) for i in range(4)]
        Rsto = [Res() for _ in range(4)]
        Rxn = [Res() for _ in range(2)]
        Rst = [Res() for _ in range(2)]
        Rstv = [Res() for _ in range(2)]
        RhnT = [Res() for _ in range(2)]
        Ru = [Res() for _ in range(2)]
        Rv = [Res() for _ in range(2)]
        Rvn = [Res() for _ in range(2)]
        Rpr = [Res() for _ in range(2)]
        Rho = [Res() for _ in range(2)]
        RT = [Res("T0"), Res("T7")]
        Tbank = [0, 7]
        RpsU2, RpsV, RpsS = [Res("psU0"), Res("psU1")], Res("psV"), Res("psS")
        psU = self.ps[:, 512:1536]
        psV = self.ps[:, 1536:2560]
        psS = self.ps[:, 2560:3584]

        def load(i):
            s3 = i % 4
            S.dma("sp", hin[s3], hin_d[i * 128:(i + 1) * 128, :], Rh[s3], reads=[Rin[i]], writes=[Rh[s3]])

        def norm0(i):
            s, s3 = i % 2, i % 4
            self.ACT(xn[s], hin[s3], AF.Square, [Rh[s3]], [Rxn[s], Rst[s]], accum_out=st[s][:, 0:1])
            self.TS("pool", st[s][:, 1:2], st[s][:, 0:1], 1.0 / D, EPS, ALU.mult, ALU.add, [Rst[s]], [Rst[s]])
            self.TT("pool", st[s][:, 2:3], st[s][:, 1:2], self.neghalf[:, 0:1], ALU.pow, [Rst[s]], [Rst[s]])

        def norm1(i):
            s, s3 = i % 2, i % 4
            self.ACT(xn[s], hin[s3], AF.Copy, [Rh[s3], Rst[s]], [Rxn[s]], scale=st[s][:, 2:3])

        def tr(i):
            s = i % 2
            self.transposes8(xn[s], Rxn[s], Tbank[s], RT[s])
            self.CP("dve", hnT[s], self.bank_bf_view(Tbank[s]), [RT[s]], [RhnT[s]])

        def stU(i):
            s = i % 2
            for c in range(8):
                for k in range(8):
                    self.MM(psU[:, c * 128:(c + 1) * 128], Wu[:, k, c * 128:(c + 1) * 128], hnT[s][:, k, :],
                            k == 0, k == 7, [RhnT[s]], [RpsU2[c // 4]], signal=(c % 4 == 3 and k == 7))
            self.ACT(u_sb[s], psU, AF.Gelu_apprx_tanh, RpsU2, [Ru[s]])

        def stV(i):
            s = i % 2
            for half in range(2):
                for k in range(8):
                    self.MM(psV[:, half * 512:(half + 1) * 512], hnT[s][:, k, :], Wv[:, k, half * 512:(half + 1) * 512],
                            k == 0, k == 7, [RhnT[s]], [RpsV], signal=(half == 1 and k == 7))
            self.ACT(v_sb[s], psV, AF.Gelu_apprx_tanh, [RpsV], [Rv[s]])
            self.ACT(vn[s], v_sb[s], AF.Square, [Rv[s]], [Rvn[s], Rstv[s]], accum_out=stv[s][:, 0:1])
            self.TS("pool", stv[s][:, 1:2], stv[s][:, 0:1], 1.0 / D, EPS, ALU.mult, ALU.add, [Rstv[s]], [Rstv[s]])
            self.TT("pool", stv[s][:, 2:3], stv[s][:, 1:2], self.neghalf[:, 0:1], ALU.pow, [Rstv[s]], [Rstv[s]])

        def stVN(i):
            s = i % 2
            self.STT(vn[s], v_sb[s], stv[s][:, 2:3], gvb, ALU.mult, ALU.mult, [Rv[s], Rstv[s]], [Rvn[s]])

        def stS(i):
            s = i % 2
            for g in range(8):
                self.MM(psS[:, g * 128:(g + 1) * 128], vn[s][:, g * 128:(g + 1) * 128], wsT[:, g, :],
                        True, False, [Rvn[s]], [RpsS], signal=False)
                self.MM(psS[:, g * 128:(g + 1) * 128], ones128, bs128[:, g, :],
                        False, True, [], [RpsS], signal=(g == 7))
            self.TT("dve", prod[s].rearrange("p a b -> p (a b)"), psS, u_sb[s], ALU.mult, [RpsS, Ru[s]], [Rpr[s]])

        def stO(i):
            s, s3 = i % 2, i % 4
            for half in range(2):
                for k in range(8):
                    self.MM(psU[:, half * 512:(half + 1) * 512], prod[s][:, k, :], Wo[:, k, half * 512:(half + 1) * 512],
                            k == 0, k == 7, [Rpr[s]], [RpsU2[half]], signal=(k == 7))
                self.TT("dve", hin[s3][:, half * 512:(half + 1) * 512], psU[:, half * 512:(half + 1) * 512],
                        hin[s3][:, half * 512:(half + 1) * 512], ALU.add, [RpsU2[half], Rh[s3]], [Rh[s3]])
            S.dma("pool", hout_d[i * 128:(i + 1) * 128, :], hin[s3], Rsto[s3], reads=[Rh[s3]], writes=[Rout[i]])

        load(0)
        load(1)
        load(2)
        for i0 in range(2):
            norm0(i0)
            norm1(i0)
            tr(i0)
        stU(0)
        stV(0)
        stVN(0)
        if prefetch_ffn:
            pf_dma(0)
            pf_dma(1)
        for i in range(NT):
            if prefetch_ffn and i < len(pieces):
                pf_cast(i)
                if i + 2 < len(pieces):
                    pf_dma(i + 2)
            if i + 3 < NT:
                load(i + 3)
            if i + 2 < NT:
                norm0(i + 2)
            if i + 1 < NT:
                stU(i + 1)
            if i + 2 < NT:
                norm1(i + 2)
            stS(i)
            if i + 1 < NT:
                stV(i + 1)
            if i + 2 < NT:
                tr(i + 2)
            stO(i)
            if i + 1 < NT:
                stVN(i + 1)

    def phase_ffn(self, hin_d, hout_d, Rin, Rout, w, l, final, win_preloaded=False, win_c0=0):
        S = self.S
        S.barrier()
        self.off = self.const_base
        Win = self.alloc([8, 2 * DFF], BF16)
        Wd = self.alloc([22, 1024], BF16)
        cw = self.alloc([44, 4], F32)
        gcol = self.alloc([8], F32)
        gfb = self.alloc([1024], F32) if final else None
        mark = self.off
        stage = [self.alloc([2048], F32) for _ in range(NSTAGE)]
        Rstage = [Res("stage%d" % i) for i in range(NSTAGE)]
        Rg = Res("gcol")
        ctr = [0]
        S.dma("sp", gcol, w["f_gcol%d" % l][:, :], Rg, writes=[Rg])
        Rc = Res("cw")
        S.dma("sp", cw.rearrange("p a b -> p (a b)"), w["f_cw%d" % l][:, :], Rc, writes=[Rc])
        if final:
            Rgf = Res("gfb")
            S.dma("sp", gfb, w["final_g"].partition_broadcast(128), Rgf, writes=[Rgf])
        if not win_preloaded:
            self.load_weight(Win, w["f_w_in%d" % l], 8, 2 * DFF, gcol, stage, Rstage, Rg, ctr, c0=win_c0)
        S.barrier()
        self.off = mark
        TT_ = 256
        NTL = NTOK // TT_
        hin = [[self.alloc([1024], F32) for _ in range(2)] for _ in range(3)]
        xn = [self.alloc([1024], BF16) for _ in range(2)]
        st = [self.alloc([4], F32) for _ in range(4)]
        hnT = [self.alloc([8, TT_ + 2], BF16) for _ in range(2)]
        accU = [self.alloc([TT_], F32) for _ in range(3)]
        accG = [self.alloc([TT_], F32) for _ in range(3)]
        gated = [self.alloc([22, TT_], BF16) for _ in range(2)]
        junk = self.alloc([1024], BF16)

        Rh = [[Res() for _ in range(2)] for _ in range(3)]
        Rsto = [[Res() for _ in range(2)] for _ in range(3)]
        Rxn = [Res() for _ in range(2)]
        Rst = [Res() for _ in range(4)]
        RhnT = [Res() for _ in range(2)]
        RaU = [Res() for _ in range(3)]
        RaG = [Res() for _ in range(3)]
        RpU = [Res() for _ in range(3)]
        RpG = [Res() for _ in range(3)]
        Rga = [[Res() for _ in range(22)] for _ in range(2)]
        Rmisc = [Res("m6"), Res("m7")]
        misc_bank = [6, 7]
        mctr = [0]
        Rjunk = Res("junk")
        NW = TT_ + 2

        def front_load(tt):
            for sub in range(2):
                t0 = tt * TT_ + sub * 128
                S.dma("sp", hin[tt % 3][sub], hin_d[t0:t0 + 128, :], Rh[tt % 3][sub], reads=[Rin[t0 // 128]],
                      writes=[Rh[tt % 3][sub]])

        def front_norm(tt, sub, part):
            si = (tt % 2) * 2 + sub
            h, Rhh = hin[tt % 3][sub], Rh[tt % 3][sub]
            if part == 0:
                self.ACT(xn[sub], h, AF.Square, [Rhh], [Rxn[sub], Rst[si]], accum_out=st[si][:, 0:1])
                self.TS("pool", st[si][:, 1:2], st[si][:, 0:1], 1.0 / D, EPS, ALU.mult, ALU.add, [Rst[si]], [Rst[si]])
                self.TT("pool", st[si][:, 2:3], st[si][:, 1:2], self.neghalf[:, 0:1], ALU.pow, [Rst[si]], [Rst[si]])
            else:
                self.ACT(xn[sub], h, AF.Copy, [Rhh, Rst[si]], [Rxn[sub]], scale=st[si][:, 2:3])

        def front_T(tt, sub):
            par = tt % 2
            m = mctr[0] % 2
            mctr[0] += 1
            self.transposes8(xn[sub], Rxn[sub], misc_bank[m], Rmisc[m])
            self.CP("dve", hnT[par][:, :, 2 + sub * 128: 2 + (sub + 1) * 128], self.bank_bf_view(misc_bank[m]),
                    [Rmisc[m]], [RhnT[par]])
            if sub == 0:
                if tt % (SEQ // TT_) == 0:
                    self.MS("pool", hnT[par][:, :, 0:2], 0.0, [], [RhnT[par]])
                else:
                    self.CP("pool", hnT[par][:, :, 0:2], hnT[1 - par][:, :, TT_:TT_ + 2], [RhnT[1 - par]], [RhnT[par]])

        def up_pe_evac_stt(tt, j):
            par = tt % 2
            b = j % 3
            psU = self.ps[:, (2 * b) * 512:(2 * b) * 512 + NW]
            psG = self.ps[:, (2 * b + 1) * 512:(2 * b + 1) * 512 + NW]
            for (pst, Rp, c) in ((psU, RpU[b], j), (psG, RpG[b], 22 + j)):
                for k in range(8):
                    self.MM(pst, Win[:, k, c * 128:(c + 1) * 128], hnT[par][:, k, :], k == 0, k == 7,
                            [RhnT[par]], [Rp], signal=(k == 7))
            for (pst, Rp, c, acc, Ra) in ((psU, RpU[b], j, accU[b], RaU[b]), (psG, RpG[b], 22 + j, accG[b], RaG[b])):
                self.ACT(acc, pst[:, 2:NW], AF.Identity, [Rp], [Ra], scale=cw[:, c, 2:3], bias=cw[:, c, 3:4])
            for (pst, Rp, c, acc, Ra) in ((psU, RpU[b], j, accU[b], RaU[b]), (psG, RpG[b], 22 + j, accG[b], RaG[b])):
                self.STT(acc, pst[:, 1:NW - 1], cw[:, c, 1:2], acc, ALU.mult, ALU.add, [Rp, Ra], [Ra])
                self.STT(acc, pst[:, 0:NW - 2], cw[:, c, 0:1], acc, ALU.mult, ALU.add, [Rp, Ra], [Ra])

        def up_silu(tt, j):
            b = j % 3
            psG = self.ps[:, (2 * b + 1) * 512:(2 * b + 1) * 512 + NW]
            self.ACT(accG[b], accG[b], AF.Silu, [RaG[b]], [RaG[b]])

        def up_prod(tt, j):
            par = tt % 2
            b = j % 3
            psG = self.ps[:, (2 * b + 1) * 512:(2 * b + 1) * 512 + NW]
            self.TT("pool", gated[par][:, j, :], accG[b], accU[b], ALU.mult, [RaG[b], RaU[b]], [Rga[par][j]])

        def down_group(tt, sub, half):
            par = tt % 2
            t0 = tt * TT_ + sub * 128
            ti = t0 // 128
            h = hin[tt % 3][sub]
            Rhh = Rh[tt % 3][sub]
            m = mctr[0] % 2
            mctr[0] += 1
            psD = self.bank(misc_bank[m])
            for k in range(22):
                self.MM(psD, gated[par][:, k, sub * 128:(sub + 1) * 128], Wd[:, k, half * 512:(half + 1) * 512],
                        k == 0, k == 21, [Rga[par][k], RWd[k]], [Rmisc[m]], signal=(k == 21))
            self.TT("dve", h[:, half * 512:(half + 1) * 512], psD, h[:, half * 512:(half + 1) * 512], ALU.add,
                    [Rmisc[m], Rhh], [Rhh])
            if half == 1 and not final:
                S.dma("pool", hout_d[t0:t0 + 128, :], h, Rsto[tt % 3][sub], reads=[Rhh], writes=[Rout[ti]])

        def final_step(tt, sub, step):
            par = tt % 2
            t0 = tt * TT_ + sub * 128
            h = hin[tt % 3][sub]
            Rhh = Rh[tt % 3][sub]
            si = par * 2 + sub
            if step == 0:
                self.ACT(junk, h, AF.Square, [Rhh], [Rjunk, Rst[si]], accum_out=st[si][:, 0:1])
                self.TS("pool", st[si][:, 1:2], st[si][:, 0:1], 1.0 / D, EPS, ALU.mult, ALU.add, [Rst[si]], [Rst[si]])
                self.TT("pool", st[si][:, 2:3], st[si][:, 1:2], self.neghalf[:, 0:1], ALU.pow, [Rst[si]], [Rst[si]])
            elif step == 1:
                self.ACT(h, h, AF.Copy, [Rhh, Rst[si]], [Rhh], scale=st[si][:, 2:3])
            elif step < 6:
                q = step - 2
                self.TT("pool", h[:, q * 256:(q + 1) * 256], h[:, q * 256:(q + 1) * 256], gfb[:, q * 256:(q + 1) * 256],
                        ALU.mult, [Rhh], [Rhh])
            else:
                S.dma("pool", hout_d[t0:t0 + 128, :], h, Rsto[tt % 3][sub], reads=[Rhh], writes=[Rout[t0 // 128]])

        RWd = [Res("wd%d" % k) for k in range(22)]

        def wd_dma(k):
            S.dma("sp", hin[2][k % 2], w["f_w_down%d" % l][k * 128:(k + 1) * 128, :], Rh[2][k % 2],
                  writes=[Rh[2][k % 2]])

        def wd_cast(k):
            self.ACT(Wd[:, k, :], hin[2][k % 2], AF.Copy, [Rh[2][k % 2]], [RWd[k]])

        DOWN_AT = {2: (0, 0), 6: (0, 1), 10: (1, 0), 14: (1, 1)} if final else {3: (0, 0), 8: (0, 1), 13: (1, 0), 18: (1, 1)}
        FINAL_AT = {7: [(0, 0)], 9: [(0, 1)], 10: [(0, 2)], 11: [(0, 3)], 12: [(0, 4)], 13: [(0, 5)], 14: [(0, 6)],
                    15: [(1, 0)], 17: [(1, 1)], 18: [(1, 2)], 19: [(1, 3)], 20: [(1, 4)], 21: [(1, 5), (1, 6)]}
        wd_dma(0)
        wd_dma(1)
        NORM_AT = {1: (0, 0), 4: (0, 1), 9: (1, 0), 12: (1, 1)}
        FRONT_AT = {8: 0, 16: 1}
        front_load(0)
        for sub in range(2):
            front_norm(0, sub, 0)
            front_norm(0, sub, 1)
            front_T(0, sub)
        for tt in range(NTL + 1):
            for j in range(22):
                if j == 0 and tt + 1 < NTL:
                    front_load(tt + 1)
                if tt < NTL:
                    up_pe_evac_stt(tt, j)
                    if j >= 1:
                        up_silu(tt, j - 1)
                        up_prod(tt, j - 1)
                if tt == 0:
                    wd_cast(j)
                    if j + 2 < 22:
                        wd_dma(j + 2)
                if tt >= 1 and j in DOWN_AT:
                    down_group(tt - 1, *DOWN_AT[j])
                if final and tt >= 1 and j in FINAL_AT:
                    for a_ in FINAL_AT[j]:
                        final_step(tt - 1, *a_)
                if j in NORM_AT and tt + 1 < NTL:
                    front_norm(tt + 1, *NORM_AT[j])
                if j in FRONT_AT and tt + 1 < NTL:
                    front_T(tt + 1, FRONT_AT[j])
            if tt < NTL:
                up_silu(tt, 21)
                up_prod(tt, 21)

    def phase_attn(self, hin_d, hout_d, Rin, Rout, w, prefetch_ffn=False):
        S = self.S
        S.barrier()
        self.off = self.const_base
        Winp = self.alloc([8, 2 * DFF], BF16)
        self.off = self.const_base
        Wk = self.alloc([8, 1024], BF16)
        Wv = self.alloc([8, 1024], BF16)
        Wq = self.alloc([8, 1024], BF16)
        Wo = self.alloc([8, 1024], BF16)
        KT = self.alloc([8, SEQ], BF16)
        V = self.alloc([16, 16, 66], BF16)
        E = self.alloc([16, 3, 128], BF16)
        cb = self.alloc([16], F32)
        ncb = self.alloc([16], F32)
        gkv = self.alloc([8], F32)
        gq = self.alloc([8], F32)
        gcolF = self.alloc([8], F32)
        Qz = [self.alloc([8, 2, 128], BF16) for _ in range(2)]
        mark = self.off
        stage = [self.alloc([2048], F32) for _ in range(4)]
        btile = self.alloc([16, 2, 128], F32)
        Rstage = [Res("stage%d" % i) for i in range(4)]
        Rg = Res("g")
        ctr = [0]
        Re = Res("E")
        Rz = Res("qz")
        self.MS("pool", V.rearrange("p a b c -> p (a b c)"), 1.0, [], [Re])
        for q in Qz:
            self.MS("pool", q.rearrange("p a b c -> p (a b c)"), 0.0, [], [Rz])
        S.dma("sp", gkv, w["kv_gcol"][:, :], Rg, writes=[Rg])
        if prefetch_ffn:
            RgF = Res("gcolF")
            S.dma("sp", gcolF, w["f_gcol1"][:, :], RgF, writes=[RgF])
        Rg2 = Res("g2")
        S.dma("sp", gq, w["b_gcol"][:, :], Rg2, writes=[Rg2])
        Rbt = Res("bt")
        S.dma("sp", btile.rearrange("p a b c -> p (a b c)"), w["btile"][:, :], Rbt, writes=[Rbt])
        Rcb = Res("cb")
        S.dma("sp", cb, w["cbias"][:, :], Rcb, writes=[Rcb])
        self.TS("dve", gq, gq, 0.125, None, ALU.mult, None, [Rg2], [Rg2])
        self.TS("dve", ncb, cb, -1.0, None, ALU.mult, None, [Rcb], [Rcb])
        Re2 = Res("E2")
        for h in range(16):
            self.ACT(E[:, h, 0:2, :], btile[:, h, :, :], AF.Exp, [Rbt, Rcb], [Re2], bias=ncb[:, h:h + 1])
        self.MS("dve", E[:, :, 2, :], 1.0, [Re2], [Re2])
        self.MS("dve", E[64:128, :, 0, 0:64], 0.0, [Re2], [Re2])
        self.MS("dve", E[0:64, :, 2, 64:128], 0.0, [Re2], [Re2])
        self.load_weight(Wk, w["w_kv"][:, 0:1024], 8, 1024, gkv, stage, Rstage, Rg, ctr)
        self.load_weight(Wv, w["w_kv"][:, 1024:2048], 8, 1024, gkv, stage, Rstage, Rg, ctr)
        self.load_weight(Wq, w["b_w_q"], 8, 1024, gq, stage, Rstage, Rg2, ctr)
        self.load_weight(Wo, w["b_w_o"], 8, 1024, None, stage, Rstage, Rg, ctr)
        S.barrier()
        self.off = mark
        hin = [self.alloc([1024], F32) for _ in range(3)]
        xn = [self.alloc([1024], BF16) for _ in range(2)]
        st = [self.alloc([4], F32) for _ in range(2)]
        hnT = [self.alloc([8, 128], BF16) for _ in range(2)]
        PT = [self.alloc([640], BF16) for _ in range(4)]
        osb = [self.alloc([132], F32) for _ in range(2)]
        O_sb = [self.alloc([1024], BF16) for _ in range(2)]
        OT = [self.alloc([8, 128], BF16) for _ in range(2)]
        rc = [self.alloc([2], F32) for _ in range(2)]
        pstage = [self.alloc([1408], F32) for _ in range(2)]
        Rps = [Res("ps0"), Res("ps1")]
        RWdead = Res("wkvq")
        pieces = [(c, n0) for c in range(4) for n0 in range(0, 2 * DFF, 1408)]

        def pf_dma(n):
            c, n0 = pieces[n]
            S.dma("sp", pstage[n % 2], w["f_w_in1"][c * 128:(c + 1) * 128, n0:n0 + 1408], Rps[n % 2],
                  writes=[Rps[n % 2]])

        def pf_cast(n):
            c, n0 = pieces[n]
            self.ACT(Winp[:, c, n0:n0 + 1408], pstage[n % 2], AF.Copy, [Rps[n % 2]], [RWdead], scale=gcolF[:, c:c + 1])

        Rh = [Res() for _ in range(3)]
        Rxn = [Res() for _ in range(2)]
        Rst = [Res() for _ in range(2)]
        RhnT = [Res() for _ in range(2)]
        RQz = [Res() for _ in range(2)]
        RPT = [Res() for _ in range(4)]
        Rosb = [Res(), Res()]
        ROs = [Res() for _ in range(2)]
        ROT = [Res() for _ in range(2)]
        Rrc = [Res() for _ in range(2)]
        RKT = [Res() for _ in range(16)]
        RV = [Res() for _ in range(16)]
        Rsto = [Res(), Res(), Res()]
        RpS = [Res("pS0"), Res("pS1"), Res("pS2")]
        RpO = [Res("pO0"), Res("pO1")]
        Rmisc = RpS
        misc_bank = [0, 2, 4]
        rot = [0]
        mctr = [0]
        pctr = [0]
        evctr = [0]
        POS = {0: 0, 1: 1, 4: 2, 2: 3, 3: 4}

        def nextm():
            b = rot[0] % 3
            rot[0] += 1
            return b

        def evac(out, in_, reads, writes, bump=True):
            if bump:
                evctr[0] += 1
            if evctr[0] % 2 == 0:
                self.ACT(out, in_, AF.Copy, reads, writes)
            else:
                self.CP("dve", out, in_, reads, writes)

        def u_load(gi):
            s3 = gi % 3
            S.dma("sp", hin[s3], hin_d[gi * 128:(gi + 1) * 128, :], Rh[s3], reads=[Rin[gi]], writes=[Rh[s3]])

        def u_norm0(gi):
            s = gi % 2
            self.ACT(xn[s], hin[gi % 3], AF.Square, [Rh[gi % 3]], [Rxn[s], Rst[s]], accum_out=st[s][:, 0:1])
            self.TS("pool", st[s][:, 1:2], st[s][:, 0:1], 1.0 / D, EPS, ALU.mult, ALU.add, [Rst[s]], [Rst[s]])
            self.TT("pool", st[s][:, 2:3], st[s][:, 1:2], self.neghalf[:, 0:1], ALU.pow, [Rst[s]], [Rst[s]])

        def u_norm1(gi):
            s = gi % 2
            self.ACT(xn[s], hin[gi % 3], AF.Copy, [Rh[gi % 3], Rst[s]], [Rxn[s]], scale=st[s][:, 2:3])

        def u_T(gi):
            s = gi % 2
            m = nextm()
            self.transposes8(xn[s], Rxn[s], misc_bank[m], Rmisc[m])
            self.CP("dve", hnT[s], self.bank_bf_view(misc_bank[m]), [Rmisc[m]], [RhnT[s]])

        def u_K(gi, hb):
            s = gi % 2
            i = gi % 16
            m = nextm()
            pm = self.bank(misc_bank[m])
            for prl in range(4):
                pr = hb * 4 + prl
                for k in range(8):
                    self.MM(pm[:, prl * 128:(prl + 1) * 128], Wk[:, k, pr * 128:(pr + 1) * 128], hnT[s][:, k, :],
                            k == 0, k == 7, [RhnT[s], RWdead], [Rmisc[m]], signal=(prl == 3 and k == 7))
            evac(KT[:, hb * 4:(hb + 1) * 4, i * 128:(i + 1) * 128], pm.rearrange("p (a b) -> p a b", a=4, b=128),
                 [Rmisc[m]], [RKT[i]])

        def u_V(gi, half):
            s = gi % 2
            i = gi % 16
            m = nextm()
            pm = self.bank(misc_bank[m])
            for k in range(8):
                self.MM(pm, hnT[s][:, k, :], Wv[:, k, half * 512:(half + 1) * 512], k == 0, k == 7,
                        [RhnT[s], RWdead], [Rmisc[m]], signal=(k == 7))
            evac(V[:, i, half * 8:(half + 1) * 8, 0:64], pm.rearrange("p (a b) -> p a b", a=8, b=64),
                 [Rmisc[m]], [RV[i]])

        def u_Q(gi, hb):
            s = gi % 2
            m = nextm()
            pm = self.bank(misc_bank[m])
            for prl in range(4):
                pr = hb * 4 + prl
                for k in range(8):
                    self.MM(pm[:, prl * 128:(prl + 1) * 128], Wq[:, k, pr * 128:(pr + 1) * 128], hnT[s][:, k, :],
                            k == 0, k == 7, [RhnT[s], RWdead], [Rmisc[m]], signal=(prl == 3 and k == 7))
            pv = pm.rearrange("p (a b) -> p a b", a=4, b=128)
            evac(Qz[s][0:64, hb * 4:(hb + 1) * 4, 0, :], pv[0:64], [Rmisc[m]], [RQz[s]])
            evac(Qz[s][64:128, hb * 4:(hb + 1) * 4, 1, :], pv[64:128], [Rmisc[m]], [RQz[s]], bump=False)

        def bg_units(gi):
            return {-1: [lambda: u_load(gi)], 0: [lambda: u_norm0(gi)],
                    2: [lambda: u_norm1(gi)],
                    4: [lambda: u_T(gi)],
                    6: [lambda: u_K(gi, 0)], 8: [lambda: u_K(gi, 1)],
                    9: [lambda: u_Q(gi, 0)], 11: [lambda: u_Q(gi, 1)],
                    13: [lambda: u_V(gi, 0)], 14: [lambda: u_V(gi, 1)]}

        def attn(gi, bg):
            s = gi % 2
            i = gi % 16
            omax = min(i, 4)
            valid = list(range(omax + 1))
            if i == 0:
                runs = [(0, 0, 0, 128)]
            elif i == 1:
                runs = [(0, 0, 0, 256)]
            elif i == 2:
                runs = [(0, 0, 0, 256), (0, 384, 384, 128)]
            elif i == 3:
                runs = [(0, 0, 0, 256), (0, 384, 384, 128), (1, 0, 512, 128)]
            else:
                runs = [(0, 0, 0, 512), (1, 0, 512, 128)]
            nmul = 128 if i == 0 else (256 if i < 4 else 384)
            state = {}

            def s_part(hd):
                pr, e = hd // 2, hd % 2
                sb_ = nextm()
                pS2 = [self.bank(2 * sb_), self.bank(2 * sb_ + 1)]
                for n, o in enumerate(valid):
                    p = POS[o]
                    dst = pS2[0][:, p * 128:(p + 1) * 128] if p < 4 else pS2[1][:, 0:128]
                    self.MM(dst, KT[:, pr, (i - o) * 128:(i - o + 1) * 128], Qz[s][:, pr, e, :],
                            True, True, [RKT[i - o], RQz[s]], [RpS[sb_]], signal=(n == len(valid) - 1))
                pb = pctr[0] % 4
                pctr[0] += 1
                state[hd] = pb
                for (src, a, pa, wd) in runs:
                    self.ACT(PT[pb][:, pa:pa + wd], pS2[src][:, a:a + wd], AF.Exp, [RpS[sb_]], [RPT[pb]],
                             bias=cb[:, hd:hd + 1])

            def m_part(hd):
                pb = state[hd]
                self.TT("dve", PT[pb][:, 0:nmul], PT[pb][:, 0:nmul],
                        E[:, hd, :, :].rearrange("p a b -> p (a b)")[:, 0:nmul], ALU.mult, [RPT[pb]], [RPT[pb]])

            def pv_part(hd):
                pr, e = hd // 2, hd % 2
                ob = pr % 2
                pb = state[hd]
                pO = self.ps[:, (6 + ob) * 512:(6 + ob) * 512 + 132]
                for n, o in enumerate(valid):
                    p = POS[o]
                    self.MM(pO[:, e * 66:e * 66 + 65], PT[pb][:, p * 128:(p + 1) * 128], V[:, i - o, hd, 0:65],
                            n == 0, n == len(valid) - 1, [RPT[pb], RV[i - o]], [RpO[ob]], signal=(n == len(valid) - 1))
                if e == 1:
                    o3 = osb[ob].rearrange("p (a b) -> p a b", a=2, b=66)
                    self.CP("dve", o3[:, :, 0:65], pO.rearrange("p (a b) -> p a b", a=2, b=66)[:, :, 0:65], [RpO[ob]], [Rosb[ob]])
                    self.S.op("dve", lambda e_: e_.reciprocal(out=rc[ob], in_=o3[:, :, 64]), [Rosb[ob]], [Rrc[ob]])
                    for ee in range(2):
                        h2 = pr * 2 + ee
                        self.TS("dve", O_sb[s][:, h2 * 64:(h2 + 1) * 64], osb[ob][:, ee * 66:ee * 66 + 64], rc[ob][:, ee:ee + 1],
                                None, ALU.mult, None, [Rosb[ob], Rrc[ob]], [ROs[s]])

            for f in bg.get(-1, []):
                f()
            s_part(0)
            m_part(0)
            s_part(1)
            m_part(1)
            for hd in range(16):
                if hd + 2 < 16:
                    s_part(hd + 2)
                pv_part(hd)
                if hd + 2 < 16:
                    m_part(hd + 2)
                for f in bg.get(hd, []):
                    f()

        def out_T(gi):
            s = gi % 2
            m = nextm()
            self.transposes8(O_sb[s], ROs[s], misc_bank[m], Rmisc[m])
            self.CP("dve", OT[s], self.bank_bf_view(misc_bank[m]), [Rmisc[m]], [ROT[s]])

        def out_wo(gi, half):
            s = gi % 2
            m = nextm()
            pm = self.bank(misc_bank[m])
            for k in range(8):
                self.MM(pm, OT[s][:, k, :], Wo[:, k, half * 512:(half + 1) * 512], k == 0, k == 7,
                        [ROT[s]], [Rmisc[m]], signal=(k == 7))
            self.TT("dve", hin[gi % 3][:, half * 512:(half + 1) * 512], pm, hin[gi % 3][:, half * 512:(half + 1) * 512], ALU.add,
                    [Rmisc[m], Rh[gi % 3]], [Rh[gi % 3]])
            if half == 1:
                S.dma("pool", hout_d[gi * 128:(gi + 1) * 128, :], hin[gi % 3], Rsto[gi % 3], reads=[Rh[gi % 3]], writes=[Rout[gi]])

        NT = NTOK // 128 if DEBUG_NT is None else DEBUG_NT
        u0 = bg_units(0)
        for key in sorted(u0):
            for f in u0[key]:
                f()
        for gi in range(NT):
            bg = bg_units(gi + 1) if gi + 1 < NT else {}
            if gi >= 1:
                bg.setdefault(0, []).insert(0, lambda g=gi - 1: out_T(g))
                bg.setdefault(1, []).append(lambda g=gi - 1: out_wo(g, 0))
                bg.setdefault(3, []).append(lambda g=gi - 1: out_wo(g, 1))
            if prefetch_ffn and gi == NT - 1 and NT == NTOK // 128:
                bg.setdefault(-1, []).extend([lambda: pf_dma(0), lambda: pf_dma(1)])
                for hd_ in range(16):
                    bg.setdefault(hd_, []).append(lambda n=hd_: pf_cast(n))
                    if hd_ + 2 < 16:
                        bg.setdefault(hd_, []).append(lambda n=hd_ + 2: pf_dma(n))
            attn(gi, bg)
        out_T(NT - 1)
        out_wo(NT - 1, 0)
        out_wo(NT - 1, 1)


PHASE_INPUTS = {
    "A": [("a_gcol", [128, 8]), ("a_w_in", [1024, 2048]), ("a_v_norm_g", [1024]), ("a_w_s", [8, 128, 128]),
          ("a_b_s", [1, 1024]), ("a_w_out", [1024, 1024])],
    "F0": [("f_gcol0", [128, 8]), ("f_cw0", [128, 176]), ("f_w_in0", [1024, 2 * DFF]), ("f_w_down0", [DFF, 1024])],
    "B": [("kv_gcol", [128, 8]), ("b_gcol", [128, 8]), ("w_kv", [1024, 2048]), ("b_w_q", [1024, 1024]),
          ("b_w_o", [1024, 1024]), ("btile", [128, 16 * 2 * 128]), ("cbias", [128, 16])],
    "F1": [("f_gcol1", [128, 8]), ("f_cw1", [128, 176]), ("f_w_in1", [1024, 2 * DFF]), ("f_w_down1", [DFF, 1024]),
           ("final_g", [1024])],
}


def build_program(phases):
    nc = bass.Bass("TRN2", target_bir_lowering=False)
    es = ExitStack()
    with es:
        hin_d = nc.dram_tensor("hin", [NTOK, D], F32, kind="ExternalInput").ap()
        hout_d = nc.dram_tensor("hout", [NTOK, D], F32, kind="ExternalOutput").ap()
        B = Builder(nc, es)
        w = {}
        for ph in phases:
            for name, shape in PHASE_INPUTS[ph]:
                w[name] = B.din(name, shape)
        NT = NTOK // 128
        cur_d, cur_R = hin_d, [Res("in%d" % i) for i in range(NT)]
        for pi, ph in enumerate(phases):
            last = pi == len(phases) - 1
            if last:
                nxt_d = hout_d
            else:
                nxt_d = nc.dram_tensor("hmid%d" % pi, [NTOK, D], F32, kind="Internal").ap()
            nxt_R = [Res("h%d_%d" % (pi, i)) for i in range(NT)]
            pre = (pi + 1 < len(phases) and phases[pi + 1] == "F0")
            if ph == "A":
                B.phase_gmlp(cur_d, nxt_d, cur_R, nxt_R, w, prefetch_ffn=pre)
            elif ph == "F0":
                B.phase_ffn(cur_d, nxt_d, cur_R, nxt_R, w, 0, False, win_preloaded=(pi > 0 and phases[pi - 1] == "A"))
            elif ph == "B":
                B.phase_attn(cur_d, nxt_d, cur_R, nxt_R, w,
                             prefetch_ffn=(pi + 1 < len(phases) and phases[pi + 1] == "F1"))
            elif ph == "F1":
                B.phase_ffn(cur_d, nxt_d, cur_R, nxt_R, w, 1, True,
                            win_c0=(4 if (pi > 0 and phases[pi - 1] == "B") else 0))
            cur_d, cur_R = nxt_d, nxt_R
        B.S.barrier()
        with nc.Block() as block:
            B.S.replay(block)
    return nc


def host_layout(inputs):
    f = lambda a: np.ascontiguousarray(np.asarray(a, dtype=np.float32))
    gcol = lambda g: f(np.asarray(g).reshape(8, 128).T)
    W = {}
    W["a_gcol"] = gcol(inputs["a_norm_g"][0])
    W["a_w_in"] = f(inputs["a_w_in"][0])
    W["a_v_norm_g"] = f(inputs["a_v_norm_g"][0])
    W["a_w_s"] = f(inputs["a_w_s"][0])
    W["a_b_s"] = f(np.asarray(inputs["a_b_s"][0]).reshape(1, 1024))
    W["a_w_out"] = f(inputs["a_w_out"][0])
    for l in range(2):
        W["f_gcol%d" % l] = gcol(inputs["f_norm_g"][l])
        cwj = np.asarray(inputs["f_conv_w"][l])
        cbj = np.asarray(inputs["f_conv_b"][l])[None, :]
        allp = np.concatenate([cwj, cbj], axis=0)
        W["f_cw%d" % l] = f(allp.reshape(4, 44, 128).transpose(2, 1, 0).reshape(128, 176))
        W["f_w_in%d" % l] = f(inputs["f_w_in"][l])
        W["f_w_down%d" % l] = f(inputs["f_w_down"][l])
    W["kv_gcol"] = gcol(inputs["kv_norm_g"])
    W["b_gcol"] = gcol(inputs["b_norm_g"][0])
    W["w_kv"] = f(inputs["w_kv"])
    W["b_w_q"] = f(inputs["b_w_q"][0])
    W["b_w_o"] = f(inputs["b_w_o"][0])
    rb = np.asarray(inputs["b_rel_bias"][0])
    k = np.arange(128)[:, None]
    q = np.arange(128)[None, :]
    idx0 = (q - k) + 128
    idx1 = np.minimum(128 + q - k, 128) + 128
    bt = np.stack([rb[:, idx0], rb[:, idx1]], axis=1)
    W["btile"] = f(bt.transpose(2, 0, 1, 3).reshape(128, 16 * 2 * 128))
    W["cbias"] = f(np.broadcast_to(rb[:, 256][None, :], (128, 16)))
    W["final_g"] = f(inputs["final_norm_g"])
    return W


_PROG_CACHE = {}


def run_phases(phases, h_shards, W):
    key = tuple(phases)
    if key not in _PROG_CACHE:
        _PROG_CACHE[key] = build_program(phases)
    nc = _PROG_CACHE[key]
    names = [n for ph in phases for n, _ in PHASE_INPUTS[ph]]
    in_maps = []
    for c in range(N_CORES):
        m = {"hin": h_shards[c]}
        for n in names:
            m[n] = W[n]
        in_maps.append(m)
    res = run_bass_kernel_spmd(nc, in_maps, core_ids=list(range(N_CORES)))
    return [r["hout"] for r in res.results]


LAUNCH_PLAN = [["A", "F0", "B", "F1"]]


def kernel(**inputs):
    x = np.ascontiguousarray(np.asarray(inputs["x"], dtype=np.float32))
    W = host_layout(inputs)
    h = [np.ascontiguousarray(x[2 * c:2 * c + 2].reshape(NTOK, D)) for c in range(N_CORES)]
    for phases in LAUNCH_PLAN:
        h = run_phases(phases, h, W)
    out = np.stack([hc.reshape(2, SEQ, D) for hc in h], axis=0).reshape(16, SEQ, D)
    return out.astype(np.float32)
```
